# Optimizing a Trainium2 kernel written in Bass

```python
import math
import jax, jax.numpy as jnp
from jax import lax
import numpy as np

D_MODEL = 2048
BATCH = 4
SEQ = 2048
DEPTH = 1

N_MEM = 256
HEAD_DIM = 128
FOX_HEADS = 8
FOX_W = FOX_HEADS * HEAD_DIM
LRU_W = D_MODEL - FOX_W
LRU_BLOCKS = 8
LRU_BLOCK = LRU_W // LRU_BLOCKS
LRU_C = 8.0
CONV_W = 4
MIX_W = FOX_W + LRU_W
IN_W = 3 * FOX_W + FOX_HEADS + 2 * LRU_W
XATT_HEADS = 4
XATT_W = XATT_HEADS * HEAD_DIM
FFN_HIDDEN = int(math.ceil((8 * D_MODEL / 3) / 256) * 256)
Q_BLOCK = 128
RMS_EPS = 1e-6

SPLITS = (FOX_W, 2 * FOX_W, 3 * FOX_W, 3 * FOX_W + FOX_HEADS, 3 * FOX_W + FOX_HEADS + LRU_W)

kernel_name = "hymba_fox_rglru_memxattn_block"


def rmsnorm(x, g):
    xf = x.astype(jnp.float32)
    y = xf * lax.rsqrt(jnp.mean(xf * xf, axis=-1, keepdims=True) + RMS_EPS)
    return (y * g.astype(jnp.float32)).astype(x.dtype)


def forgetting_attention(q, k, v, c):
    B, H, S, dh = q.shape
    nb = S // Q_BLOCK
    scale = 1.0 / math.sqrt(dh)
    qb = q.reshape(B, H, nb, Q_BLOCK, dh).transpose(2, 0, 1, 3, 4)
    cb = c.reshape(B, H, nb, Q_BLOCK).transpose(2, 0, 1, 3)
    k_pos = jnp.arange(S)

    def one_block(args):
        q_i, c_i, i = args
        s = jnp.einsum('bhqd,bhkd->bhqk', q_i, k, preferred_element_type=jnp.float32) * scale
        s = s + c_i[..., None] - c[:, :, None, :]
        q_pos = i * Q_BLOCK + jnp.arange(Q_BLOCK)
        causal = k_pos[None, :] <= q_pos[:, None]
        s = jnp.where(causal, s, -jnp.inf)
        p = jax.nn.softmax(s, axis=-1)
        return jnp.einsum('bhqk,bhkd->bhqd', p.astype(v.dtype), v)

    o = lax.map(one_block, (qb, cb, jnp.arange(nb)))
    return o.transpose(1, 2, 0, 3, 4).reshape(B, H, S, dh)


def causal_depthwise_conv(u, w, b):
    S = u.shape[1]
    up = jnp.pad(u, ((0, 0), (CONV_W - 1, 0), (0, 0)))
    return b + sum(w[j] * up[:, j:j + S] for j in range(CONV_W))


def rg_lru(u, w_ra, b_ra, w_ri, b_ri, lam):
    B, S, W = u.shape
    ub = u.reshape(B, S, LRU_BLOCKS, LRU_BLOCK)
    r = jax.nn.sigmoid(jnp.einsum('bsnc,ncd->bsnd', ub, w_ra).reshape(B, S, W) + b_ra)
    i = jax.nn.sigmoid(jnp.einsum('bsnc,ncd->bsnd', ub, w_ri).reshape(B, S, W) + b_ri)
    log_a = -LRU_C * r.astype(jnp.float32) * jax.nn.softplus(-lam.astype(jnp.float32))
    a = jnp.exp(log_a)
    b_in = jnp.sqrt(-jnp.expm1(2.0 * log_a)) * (i * u).astype(jnp.float32)

    def combine(left, right):
        a1, b1 = left
        a2, b2 = right
        return a1 * a2, a2 * b1 + b2

    _, h = lax.associative_scan(combine, (a, b_in), axis=1)
    return h.astype(u.dtype)


def setup_inputs(seed: int = 0) -> dict:
    key = jax.random.key(seed)
    ks = jax.random.split(key, 32)
    f32 = jnp.float32

    def nrm(k, shape, scale):
        return jax.random.normal(k, shape, f32) * scale

    def gain(k, shape):
        return 1.0 + 0.02 * jax.random.normal(k, shape, f32)

    L = DEPTH
    a_c = jax.random.uniform(ks[13], (L, LRU_W), f32, 0.9, 0.999)
    s_lam = a_c ** (1.0 / LRU_C)
    lam = jnp.log(s_lam) - jnp.log1p(-s_lam)
    return {
        "x": nrm(ks[0], (BATCH, SEQ, D_MODEL), 1.0),
        "mem": nrm(ks[1], (BATCH, N_MEM, D_MODEL), 1.0),
        "g_mix": gain(ks[2], (L, D_MODEL)),
        "w_in": nrm(ks[3], (L, D_MODEL, IN_W), D_MODEL ** -0.5),
        "b_f": jax.random.uniform(ks[4], (L, FOX_HEADS), f32, 3.0, 5.0),
        "g_q": gain(ks[5], (L, HEAD_DIM)),
        "g_k": gain(ks[6], (L, HEAD_DIM)),
        "conv_w": nrm(ks[7], (L, CONV_W, LRU_W), CONV_W ** -0.5),
        "conv_b": nrm(ks[8], (L, LRU_W), 0.02),
        "w_ra": nrm(ks[9], (L, LRU_BLOCKS, LRU_BLOCK, LRU_BLOCK), LRU_BLOCK ** -0.5),
        "b_ra": nrm(ks[10], (L, LRU_W), 0.02),
        "w_ri": nrm(ks[11], (L, LRU_BLOCKS, LRU_BLOCK, LRU_BLOCK), LRU_BLOCK ** -0.5),
        "b_ri": nrm(ks[12], (L, LRU_W), 0.02),
        "lam": lam,
        "g_fox_out": gain(ks[14], (L, FOX_W)),
        "g_lru_out": gain(ks[15], (L, LRU_W)),
        "w_out": nrm(ks[16], (L, MIX_W, D_MODEL), MIX_W ** -0.5),
        "g_xattn": gain(ks[17], (L, D_MODEL)),
        "g_mem": gain(ks[18], (L, D_MODEL)),
        "w_cq": nrm(ks[19], (L, D_MODEL, XATT_W), D_MODEL ** -0.5),
        "w_ckv": nrm(ks[20], (L, D_MODEL, 2 * XATT_W), D_MODEL ** -0.5),
        "g_cq": gain(ks[21], (L, HEAD_DIM)),
        "g_ck": gain(ks[22], (L, HEAD_DIM)),
        "w_co": nrm(ks[23], (L, XATT_W, D_MODEL), XATT_W ** -0.5),
        "g_ffn": gain(ks[24], (L, D_MODEL)),
        "w_gate_up": nrm(ks[25], (L, D_MODEL, 2 * FFN_HIDDEN), D_MODEL ** -0.5),
        "w_down": nrm(ks[26], (L, FFN_HIDDEN, D_MODEL), FFN_HIDDEN ** -0.5),
    }


def reference(x, mem, g_mix, w_in, b_f, g_q, g_k, conv_w, conv_b, w_ra, b_ra, w_ri, b_ri,
              lam, g_fox_out, g_lru_out, w_out, g_xattn, g_mem, w_cq, w_ckv, g_cq, g_ck,
              w_co, g_ffn, w_gate_up, w_down):
    B, S, _ = x.shape
    M = mem.shape[1]
    for l in range(DEPTH):
        h = rmsnorm(x, g_mix[l])
        proj = h @ w_in[l]
        q, k, v, f_logit, u, gate = jnp.split(proj, SPLITS, axis=-1)
        q = rmsnorm(q.reshape(B, S, FOX_HEADS, HEAD_DIM), g_q[l]).transpose(0, 2, 1, 3)
        k = rmsnorm(k.reshape(B, S, FOX_HEADS, HEAD_DIM), g_k[l]).transpose(0, 2, 1, 3)
        v = v.reshape(B, S, FOX_HEADS, HEAD_DIM).transpose(0, 2, 1, 3)
        log_f = jax.nn.log_sigmoid((f_logit + b_f[l]).astype(jnp.float32))
        c = lax.cumsum(log_f, axis=1).transpose(0, 2, 1)
        o_fox = forgetting_attention(q, k, v, c)
        o_fox = o_fox.transpose(0, 2, 1, 3).reshape(B, S, FOX_W)

        u = causal_depthwise_conv(u, conv_w[l], conv_b[l])
        y_lru = rg_lru(u, w_ra[l], b_ra[l], w_ri[l], b_ri[l], lam[l]) * jax.nn.gelu(gate)

        mix = jnp.concatenate([rmsnorm(o_fox, g_fox_out[l]), rmsnorm(y_lru, g_lru_out[l])], axis=-1)
        x = x + mix @ w_out[l]

        hq = rmsnorm(x, g_xattn[l])
        mn = rmsnorm(mem, g_mem[l])
        cq = rmsnorm((hq @ w_cq[l]).reshape(B, S, XATT_HEADS, HEAD_DIM), g_cq[l])
        ck, cv = jnp.split(mn @ w_ckv[l], 2, axis=-1)
        ck = rmsnorm(ck.reshape(B, M, XATT_HEADS, HEAD_DIM), g_ck[l])
        cv = cv.reshape(B, M, XATT_HEADS, HEAD_DIM)
        s = jnp.einsum('bshd,bmhd->bhsm', cq, ck, preferred_element_type=jnp.float32) / math.sqrt(HEAD_DIM)
        p = jax.nn.softmax(s, axis=-1)
        o_x = jnp.einsum('bhsm,bmhd->bshd', p.astype(cv.dtype), cv).reshape(B, S, XATT_W)
        x = x + o_x @ w_co[l]

        hf = rmsnorm(x, g_ffn[l])
        f_gate, f_up = jnp.split(hf @ w_gate_up[l], 2, axis=-1)
        x = x + (jax.nn.silu(f_gate) * f_up) @ w_down[l]
    return x
```

```python
import math
from contextlib import ExitStack

import numpy as np
import concourse.bass as bass
import concourse.mybir as mybir
from concourse.bass_utils import run_bass_kernel_spmd

F32 = mybir.dt.float32
BF16 = mybir.dt.bfloat16
AF = mybir.ActivationFunctionType
ALU = mybir.AluOpType

D = 2048
NTOK = 2048
OWN = 1024
IN_W = 5128
FFN = 5632
NHC = 44
EPS = 1e-6
KFOLD = True
KSCALE = True

C_GQ, C_GK, C_GCQ, C_GCK, C_PM = 0, 1, 2, 3, 4
C_CW = 5
C_CB = 37
C_BRA = 45
C_BRI = 53
C_LAM = 61
C_GFOX = 69
C_GLRU = 77
NV = 85
C_GQS, C_GCQS, C_PMNEG, C_EPS, C_ONE = 85, 86, 87, 88, 89
C_SC1 = 90
C_TMP = 98
C_HBRA, C_HBRI, C_HSC1, C_NHSC1, C_HPM = 130, 138, 146, 154, 162
C_HGLRU, C_RGFOX, C_RGLRU = 168, 176, 184
PVW = 192


class Eng:
    def __init__(self, name, h, sem):
        self.name, self.h, self.sem, self.cnt, self.seen = name, h, sem, 0, {}


class Tile:
    __slots__ = ("name", "w", "r", "dsem", "dcnt")

    def __init__(self, name):
        self.name, self.w, self.r, self.dsem, self.dcnt = name, None, {}, None, 0


class KB:
    def __init__(self, nc, es):
        self.nc, self.es = nc, es
        mk = lambda n: es.enter_context(nc.semaphore(n))
        self.pe = Eng("pe", nc.tensor, mk("s_pe"))
        self.act = Eng("act", nc.scalar, mk("s_act"))
        self.dve = Eng("dve", nc.vector, mk("s_dve"))
        self.pool = Eng("pool", nc.gpsimd, mk("s_pool"))
        self.sp = Eng("sp", nc.sync, mk("s_sp"))
        self.engs = [self.pe, self.act, self.dve, self.pool, self.sp]
        self.free_sems = []
        self.dma_tiles = []
        self.nsem = 0

    def _wait(self, eng, deps):
        for sem, val in deps:
            k = id(sem)
            if eng.seen.get(k, 0) >= val:
                continue
            eng.h.wait_ge(sem, val)
            eng.seen[k] = val

    def _deps(self, eng, reads, writes, skip=None):
        d = {}

        def add(sv):
            if sv is None:
                return
            sem, val = sv
            if sem is skip:
                return
            k = id(sem)
            if k not in d or d[k][1] < val:
                d[k] = (sem, val)

        for t in reads:
            add(t.w)
        for t in writes:
            if t.w is not None and t.w[0] is not eng.sem:
                add(t.w)
            for sv in t.r.values():
                if sv[0] is not eng.sem:
                    add(sv)
        return list(d.values())

    def _mark(self, reads, writes, st):
        for t in reads:
            t.r[id(st[0])] = st
        for t in writes:
            t.w = st
            t.r = {}

    def op(self, eng, fn, reads=(), writes=()):
        self._wait(eng, self._deps(eng, reads, writes))
        ins = fn()
        eng.cnt += 1
        ins.then_inc(eng.sem, 1)
        self._mark(reads, writes, (eng.sem, eng.cnt))

    def mm(self, out_t, out_ap, pairs, reads, start=True, stop=True):
        eng = self.pe
        self._wait(eng, self._deps(eng, reads, [out_t]))
        n = len(pairs)
        ins = None
        for i, (l, r) in enumerate(pairs):
            ins = self.nc.tensor.matmul(out_ap, lhsT=l, rhs=r, start=(start and i == 0), stop=(stop and i == n - 1))
        eng.cnt += 1
        ins.then_inc(eng.sem, 1)
        self._mark(reads, [out_t], (eng.sem, eng.cnt))

    def mm_multi(self, out_t, segs, reads, start=True, stop=True, first_only=False):
        eng = self.pe
        self._wait(eng, self._deps(eng, reads, [out_t]))
        ins = None
        for si_, (out_ap, pairs) in enumerate(segs):
            n = len(pairs)
            for i, (l, r) in enumerate(pairs):
                st_ = start and i == 0 and (si_ == 0 or not first_only)
                ins = self.nc.tensor.matmul(out_ap, lhsT=l, rhs=r, start=st_, stop=(stop and i == n - 1))
        eng.cnt += 1
        ins.then_inc(eng.sem, 1)
        self._mark(reads, [out_t], (eng.sem, eng.cnt))

    def _getsem(self):
        if self.free_sems:
            return self.free_sems.pop()
        self.nsem += 1
        return (self.es.enter_context(self.nc.semaphore(f"s_d{self.nsem}")), 0)

    def dma_in(self, q, out_t, out_ap, in_ap):
        if out_t.dsem is None:
            out_t.dsem, out_t.dcnt = self._getsem()
            self.dma_tiles.append(out_t)
        self._wait(q, self._deps(q, [], [out_t], skip=out_t.dsem))
        ins = q.h.dma_start(out=out_ap, in_=in_ap)
        out_t.dcnt += 16
        ins.then_inc(out_t.dsem, 16)
        out_t.w = (out_t.dsem, out_t.dcnt)
        out_t.r = {}

    def dma_out(self, q, in_t, out_ap, in_ap):
        if in_t.dsem is None:
            in_t.dsem, in_t.dcnt = self._getsem()
            self.dma_tiles.append(in_t)
        self._wait(q, self._deps(q, [in_t], []))
        ins = q.h.dma_start(out=out_ap, in_=in_ap)
        in_t.dcnt += 16
        ins.then_inc(in_t.dsem, 16)
        in_t.r[id(in_t.dsem)] = (in_t.dsem, in_t.dcnt)

    def barrier(self):
        deps = [(e.sem, e.cnt) for e in self.engs if e.cnt > 0]
        deps += [(t.dsem, t.dcnt) for t in self.dma_tiles if t.dcnt > 0]
        for e in self.engs:
            self._wait(e, deps)


def build(debug=False, stop_after=None):
    nc = bass.Bass("TRN2", target_bir_lowering=False)

    def din(name, shape):
        return nc.dram_tensor(name, shape, F32, kind="ExternalInput").ap()

    xin = din("xin", [NTOK, D])
    memb = din("memb", [256, D])
    pvec = din("pvec", [128, NV])
    grow = din("grow", [4, D])
    bfr = din("bfr", [1, 128])
    w_in = din("w_in", [D, IN_W])
    w_ra = din("w_ra", [8, 128, 128])
    w_ri = din("w_ri", [8, 128, 128])
    w_out = din("w_out", [D, D])
    w_cq = din("w_cq", [D, 512])
    w_ckv = din("w_ckv", [D, 1024])
    w_co = din("w_co", [512, D])
    w_gu = din("w_gu", [D, 2 * FFN])
    w_down = din("w_down", [FFN, D])
    out = nc.dram_tensor("out", [OWN, D], F32, kind="ExternalOutput").ap()
    dbg_out = {}

    w_in_v = w_in.rearrange("(kc p) n -> p kc n", p=128)
    w_out_v = w_out.rearrange("(kc p) n -> p kc n", p=128)
    w_cq_v = w_cq.rearrange("(kc p) n -> p kc n", p=128)
    w_ckv_v = w_ckv.rearrange("(kc p) n -> p kc n", p=128)
    w_co_v = w_co.rearrange("(kc p) n -> p kc n", p=128)
    w_gu_v = w_gu.rearrange("(kc p) n -> p kc n", p=128)
    w_down_v = w_down.rearrange("(hc p) n -> p hc n", p=128)
    out_v = out.rearrange("(tb p) n -> p tb n", p=128)

    with ExitStack() as es:
        CONST_SZ, RX_SZ, RH_SZ, RS_SZ = 1856, 16384, 16384, 18500
        ARENA_F = CONST_SZ + RX_SZ + RH_SZ + RS_SZ
        arena = es.enter_context(nc.sbuf_tensor("arena", [128, ARENA_F], F32))
        psb = [es.enter_context(nc.psum_tensor(f"psb{i}", [128, 512], F32)) for i in range(8)]
        ps = [p[:, :] for p in psb]
        ps_t = [Tile(f"ps{i}") for i in range(8)]
        K = KB(nc, es)
        pe, act, dve, pool, sp = K.pe, K.act, K.dve, K.pool, K.sp
        V_, S_, G_ = nc.vector, nc.scalar, nc.gpsimd

        class Region:
            def __init__(self, base, size):
                self.base, self.size, self.off = base, size, 0

            def reset(self, to=0):
                self.off = to

            def alloc(self, shape, dt):
                n = int(np.prod(shape[1:]))
                nf = n if dt == F32 else (n + 1) // 2
                nf = (nf + 7) // 8 * 8
                assert self.off + nf <= self.size, ("SBUF region overflow", self.base, self.off, nf, self.size)
                a0 = self.base + self.off
                ap = arena[:, a0:a0 + nf]
                self.off += nf
                if dt != F32:
                    ap = ap.bitcast(dt)
                ap = ap[:, 0:n]
                if len(shape) == 3:
                    ap = ap.rearrange("p (a b) -> p a b", a=shape[1])
                elif len(shape) == 4:
                    ap = ap.rearrange("p (a b c) -> p a b c", a=shape[1], b=shape[2])
                return ap

        RC = Region(0, CONST_SZ)
        RX = Region(CONST_SZ, RX_SZ)
        RH = Region(CONST_SZ + RX_SZ, RH_SZ)
        RS = Region(CONST_SZ + RX_SZ + RH_SZ, RS_SZ)
        alloc = RC.alloc

        def bf(psap):
            return psap.bitcast(BF16)

        pv = alloc([128, PVW], F32); pv_t = Tile("pv")
        ident = alloc([128, 128], BF16)
        onesb = alloc([128, 128], BF16)
        onesf = alloc([128, 128], F32)
        triU = alloc([128, 128], F32)
        sel0 = alloc([128, 128], F32)
        masks = alloc([128, 4, 512], BF16)
        bfb = alloc([128, 128], F32)
        cst_t = Tile("consts")
        col = lambda c: pv[:, c:c + 1]

        K.dma_in(sp, pv_t, pv[:, 0:NV], pvec[:, :])
        bfb_t = Tile("bfb")
        K.dma_in(sp, bfb_t, bfb, bfr[0:1, :].partition_broadcast(128))
        K.op(pool, lambda: G_.memset(onesf, 1.0), [], [cst_t])
        K.op(pool, lambda: G_.memset(onesb, 1.0), [], [cst_t])
        K.op(pool, lambda: G_.memset(masks, 0.0), [], [cst_t])
        K.op(pool, lambda: G_.affine_select(out=ident, in_=onesb, pattern=[[-1, 128]], compare_op=ALU.is_equal,
                                            fill=0.0, base=0, channel_multiplier=1), [cst_t], [cst_t])
        K.op(pool, lambda: G_.affine_select(out=triU, in_=onesf, pattern=[[1, 128]], compare_op=ALU.is_ge,
                                            fill=0.0, base=0, channel_multiplier=-1), [cst_t], [cst_t])
        K.op(pool, lambda: G_.affine_select(out=sel0, in_=onesf, pattern=[[0, 128]], compare_op=ALU.is_ge,
                                            fill=0.0, base=0, channel_multiplier=-1), [cst_t], [cst_t])
        for r in range(4):
            K.op(pool, lambda r=r: G_.affine_select(out=masks[:, r, :], in_=masks[:, r, :], pattern=[[1, 512]],
                                                    compare_op=ALU.is_ge, fill=-30000.0, base=-128 * r,
                                                    channel_multiplier=-1), [cst_t], [cst_t])
        pv2_t = Tile("pv2")
        K.op(pool, lambda: G_.memset(col(C_EPS), EPS), [], [pv2_t])
        K.op(pool, lambda: G_.memset(col(C_ONE), 1.0), [], [pv2_t])
        sc = 1.0 / math.sqrt(128.0)
        K.op(dve, lambda: V_.tensor_scalar_mul(out=col(C_GQS), in0=col(C_GQ), scalar1=sc), [pv_t], [pv2_t])
        K.op(dve, lambda: V_.tensor_scalar_mul(out=col(C_GCQS), in0=col(C_GCQ), scalar1=sc), [pv_t], [pv2_t])
        K.op(dve, lambda: V_.tensor_scalar(out=col(C_PMNEG), in0=col(C_PM), scalar1=-1.0, scalar2=30000.0,
                                           op0=ALU.add, op1=ALU.mult), [pv_t], [pv2_t])
        def log1p_series(z, z_t, out_, out_t, w, w_t, p, p_t):
            K.op(dve, lambda: V_.tensor_scalar_add(out=w, in0=z, scalar1=2.0), [z_t], [w_t])
            K.op(dve, lambda: V_.reciprocal(out=w, in_=w), [w_t], [w_t])
            K.op(dve, lambda: V_.tensor_tensor(out=w, in0=w, in1=z, op=ALU.mult), [w_t, z_t], [w_t])
            K.op(dve, lambda: V_.tensor_tensor(out=out_, in0=w, in1=w, op=ALU.mult), [w_t], [out_t])
            K.op(dve, lambda: V_.tensor_scalar_mul(out=p, in0=out_, scalar1=1.0 / 9), [out_t], [p_t])
            for c_ in (1.0 / 7, 1.0 / 5, 1.0 / 3):
                K.op(dve, lambda: V_.scalar_tensor_tensor(out=p, in0=p, scalar=c_, in1=out_, op0=ALU.add, op1=ALU.mult),
                     [p_t, out_t], [p_t])
            K.op(dve, lambda: V_.scalar_tensor_tensor(out=p, in0=p, scalar=1.0, in1=w, op0=ALU.add, op1=ALU.mult),
                 [p_t, w_t], [p_t])
            K.op(dve, lambda: V_.tensor_scalar_mul(out=out_, in0=p, scalar1=2.0), [p_t], [out_t])

        def softplus_neg(x, x_t, out_, out_t, e, e_t, w, w_t, p, p_t):
            K.op(act, lambda: S_.activation(out=e, in_=x, func=AF.Abs), [x_t], [e_t])
            K.op(act, lambda: S_.activation(out=e, in_=e, func=AF.Exp, scale=-1.0), [e_t], [e_t])
            log1p_series(e, e_t, out_, out_t, w, w_t, p, p_t)
            K.op(dve, lambda: V_.tensor_scalar(out=w, in0=x, scalar1=-1.0, scalar2=0.0, op0=ALU.mult, op1=ALU.max),
                 [x_t], [w_t])
            K.op(dve, lambda: V_.tensor_tensor(out=out_, in0=out_, in1=w, op=ALU.add), [out_t, w_t], [out_t])

        tA, tB, tC, tD = (pv[:, C_TMP + 8 * i:C_TMP + 8 * i + 8] for i in range(4))
        tA_t, tB_t, tC_t, tD_t = Tile("tA"), Tile("tB"), Tile("tC"), Tile("tD")
        softplus_neg(pv[:, C_LAM:C_LAM + 8], pv_t, tA, tA_t, tB, tB_t, tC, tC_t, tD, tD_t)
        K.op(dve, lambda: V_.tensor_scalar_mul(out=pv[:, C_SC1:C_SC1 + 8], in0=tA, scalar1=-8.0), [tA_t], [pv2_t])
        K.op(dve, lambda: V_.tensor_scalar_mul(out=pv[:, C_HSC1:C_HSC1 + 8], in0=tA, scalar1=-4.0), [tA_t], [pv2_t])
        K.op(dve, lambda: V_.tensor_scalar_mul(out=pv[:, C_NHSC1:C_NHSC1 + 8], in0=tA, scalar1=4.0), [tA_t], [pv2_t])
        K.op(dve, lambda: V_.tensor_scalar_mul(out=pv[:, C_HBRA:C_HBRA + 8], in0=pv[:, C_BRA:C_BRA + 8], scalar1=0.5),
             [pv_t], [pv2_t])
        K.op(dve, lambda: V_.tensor_scalar_mul(out=pv[:, C_HBRI:C_HBRI + 8], in0=pv[:, C_BRI:C_BRI + 8], scalar1=0.5),
             [pv_t], [pv2_t])
        K.op(dve, lambda: V_.tensor_scalar_mul(out=col(C_HPM), in0=col(C_PM), scalar1=0.5), [pv_t], [pv2_t])
        K.op(dve, lambda: V_.tensor_scalar_mul(out=pv[:, C_HGLRU:C_HGLRU + 8], in0=pv[:, C_GLRU:C_GLRU + 8], scalar1=0.5),
             [pv_t], [pv2_t])
        K.op(dve, lambda: V_.reciprocal(out=pv[:, C_RGFOX:C_RGFOX + 16], in_=pv[:, C_GFOX:C_GFOX + 16]), [pv_t], [pv2_t])
        PVR = [pv_t, pv2_t]

        rot = {}

        def nxt(key, n):
            rot[key] = (rot.get(key, -1) + 1) % n
            return rot[key]

        def norm_rows(src_t, src_ap, gbc_t, gbc, junk, stat, stat_t, hn, hn_t):
            K.op(act, lambda: S_.activation(out=junk, in_=src_ap, func=AF.Square, accum_out=stat[:, 0:1]),
                 [src_t], [stat_t, hn_t])
            K.op(act, lambda: S_.activation(out=stat[:, 1:2], in_=stat[:, 0:1], func=AF.Sqrt, scale=1.0 / D,
                                            bias=col(C_EPS)), [stat_t] + PVR, [stat_t])
            K.op(dve, lambda: V_.reciprocal(out=stat[:, 2:3], in_=stat[:, 1:2]), [stat_t], [stat_t])
            K.op(dve, lambda: V_.scalar_tensor_tensor(out=hn, in0=src_ap, scalar=stat[:, 2:3], in1=gbc,
                                                      op0=ALU.mult, op1=ALU.mult), [src_t, stat_t, gbc_t], [hn_t])

        def transpose_rows(hn_t, hn, dstT, dst_t, c0):
            bp = 4 + 2 * nxt("trbank", 2)
            for half in range(2):
                b = bp + half
                pt = bf(ps[b])
                K._wait(pe, K._deps(pe, [hn_t, cst_t], [ps_t[b]]))
                ins = None
                for k8 in range(8):
                    kc = half * 8 + k8
                    ins = nc.tensor.transpose(pt[:, k8 * 128:(k8 + 1) * 128], hn[:, kc * 128:(kc + 1) * 128], ident)
                pe.cnt += 1
                ins.then_inc(pe.sem, 1)
                K._mark([hn_t, cst_t], [ps_t[b]], (pe.sem, pe.cnt))
                src = pt.rearrange("p (a b) -> p a b", a=8)
                dst = dstT[:, half * 8:(half + 1) * 8, c0:c0 + 128]
                if half == 0:
                    K.op(act, lambda: S_.copy(out=dst, in_=src), [ps_t[b]], [dst_t])
                else:
                    K.op(dve, lambda: V_.tensor_copy(out=dst, in_=src), [ps_t[b]], [dst_t])

        HN = {}

        def headnorm(pb, n, dst_t, dst_ap, gcol, inv_d):
            k_ = nxt("hnbuf", 2)
            sq, sq_t, sr, sr_t = HN["sq"][k_], HN["sq_t"][k_], HN["sr"][k_], HN["sr_t"][k_]
            K.op(act, lambda: S_.activation(out=sq[:, 0:n], in_=ps[pb][:, 0:n], func=AF.Square), [ps_t[pb]], [sq_t])
            K.mm(ps_t[5], ps[5][:, 0:n], [(onesb, sq[:, 0:n])], [sq_t, cst_t])
            K.op(act, lambda: S_.activation(out=sr[:, 0:n], in_=ps[5][:, 0:n], func=AF.Sqrt, scale=inv_d,
                                            bias=col(C_EPS)), [ps_t[5]] + PVR, [sr_t])
            K.op(dve, lambda: V_.reciprocal(out=sr[:, 0:n], in_=sr[:, 0:n]), [sr_t], [sr_t])
            K.op(dve, lambda: V_.scalar_tensor_tensor(out=dst_ap, in0=ps[pb][:, 0:n], scalar=col(gcol), in1=sr[:, 0:n],
                                                      op0=ALU.mult, op1=ALU.mult), [ps_t[pb], sr_t] + PVR, [dst_t])

        def dump(name, ap, shape, dt, tiles):
            if not debug:
                return
            d = nc.dram_tensor("dbg_" + name, shape, dt, kind="ExternalOutput").ap()
            dbg_out[name] = (shape, dt)
            t = Tile("dbgsrc_" + name)
            K._wait(sp, K._deps(sp, tiles, []))
            K.dma_out(sp, t, d, ap)
            for tt in tiles:
                tt.r[id(t.dsem)] = (t.dsem, t.dcnt)

        hT = RH.alloc([128, 16, NTOK], BF16); hT_t = [Tile(f"hT{i}") for i in range(16)]
        gbc = RX.alloc([128, D], F32); gbc_t = Tile("gbc")
        xst = [RX.alloc([128, D], F32) for _ in range(2)]; xst_t = [Tile(f"xst{i}") for i in range(2)]
        hn = [RX.alloc([128, D], BF16) for _ in range(2)]; hn_t = [Tile(f"hn{i}") for i in range(2)]
        assert RX.off <= 8192
        RX.reset(8192)
        oT = RX.alloc([128, 8, OWN], BF16); oT_t = [Tile(f"oT{i}") for i in range(8)]
        yT = RX.alloc([128, 8, OWN], BF16); yT_t = [Tile(f"yT{i}") for i in range(8)]
        wbufA = [RS.alloc([128, 16 * 384], BF16) for _ in range(2)]; wbufA_t = [Tile(f"wbA{i}") for i in range(2)]
        wbufL = [RS.alloc([128, 16 * 256], BF16) for _ in range(2)]; wbufL_t = [Tile(f"wbL{i}") for i in range(2)]
        KT = RS.alloc([128, NTOK], BF16); KT_t = [Tile(f"KT{i}") for i in range(4)]
        Vt = RS.alloc([128, 16, 128], BF16); V_t = [Tile(f"V{i}") for i in range(4)]
        QT = RS.alloc([128, OWN], BF16); QT_t = [Tile(f"QT{i}") for i in range(2)]
        PT = [RS.alloc([128, 512], BF16) for _ in range(3)]; PT_t = [Tile(f"PT{i}") for i in range(3)]
        HN["sq"] = [RS.alloc([128, 512], BF16) for _ in range(2)]; HN["sq_t"] = [Tile(f"sq{i}") for i in range(2)]
        HN["sr"] = [RS.alloc([128, 512], F32) for _ in range(2)]; HN["sr_t"] = [Tile(f"sr{i}") for i in range(2)]
        wf = RS.alloc([128, 16, 8], BF16); wf_t = Tile("wf")
        lneg = RS.alloc([128, 16, 8], F32); lneg_t = Tile("lneg")
        negc = RS.alloc([128, 16, 8], F32); negc_t = Tile("negc")
        cref = RS.alloc([128, 16], F32); cref_t = Tile("cref")
        biasall = RS.alloc([128, 2, 16, 8], F32); bias_t = Tile("biasall")
        stat = [RS.alloc([128, 8], F32) for _ in range(2)]; stat_t = [Tile(f"stat{i}") for i in range(2)]

        xst = xst + [wbufL[0].bitcast(F32), wbufL[1].bitcast(F32)]
        xst_t = xst_t + [Tile("xst2"), Tile("xst3")]
        hn = hn + [KT[:, 0:D], Vt.rearrange("p a b -> p (a b)")]
        hn_t = hn_t + [Tile("hn2"), Tile("hn3")]
        stat4 = [stat[0][:, 0:4], stat[0][:, 4:8], stat[1][:, 0:4], stat[1][:, 4:8]]
        stat4_t = [Tile(f"stat4_{i}") for i in range(4)]
        K.dma_in(sp, gbc_t, gbc, grow[0:1, :].partition_broadcast(128))
        K.dma_in(pool, wf_t, wf, w_in_v[:, :, 3072:3080])
        def nr1a(tb):
            i = tb % 4
            K.dma_in(sp, xst_t[i], xst[i], xin[tb * 128:(tb + 1) * 128, :])
            norm_rows(xst_t[i], xst[i], gbc_t, gbc, hn[i], stat4[i], stat4_t[i], hn[i], hn_t[i])

        SK = 2
        for tb in range(SK):
            nr1a(tb)
        for tb in range(16):
            if tb + SK < 16:
                nr1a(tb + SK)
            transpose_rows(hn_t[tb % 4], hn[tb % 4], hT, hT_t[tb], tb * 128)
        xst, xst_t, hn, hn_t = xst[:2], xst_t[:2], hn[:2], hn_t[:2]
        dump("hT", hT, [128, 16, NTOK], BF16, hT_t)

        def hsl(kc, tc):
            return hT[:, kc, tc * 512:(tc + 1) * 512]

        def hts(tc):
            return hT_t[4 * tc:4 * tc + 4]

        _wA0 = wbufA[0].rearrange("p (a b) -> p a b", a=16)
        for j3 in range(3):
            K.dma_in(pool, wbufA_t[0], _wA0[:, :, j3 * 128:(j3 + 1) * 128], w_in_v[:, :, j3 * 1024:j3 * 1024 + 128])
        rot["wbA"] = 0
        ldA0 = (_wA0, wbufA_t[0])

        K.barrier()
        RX.reset(0)
        ur = [RX.alloc([128, 520], F32) for _ in range(2)]; ur_t = [Tile(f"ur{i}") for i in range(2)]
        LS = []
        for k_ in range(2):
            d_ = {}
            for nm in ("uc", "thr", "thi", "aa", "tx"):
                d_[nm] = RX.alloc([128, 512], F32)
                d_[nm + "_t"] = Tile(f"{nm}{k_}")
            LS.append(d_)
        hs = [RX.alloc([128, 512], F32) for _ in range(2)]; hs_t = [Tile(f"hs{i}") for i in range(2)]
        rsk = [RX.alloc([128, 128], F32) for _ in range(2)]; rsk_t = [Tile(f"rsk{i}") for i in range(2)]
        assert RX.off <= 8192, RX.off
        g1 = [RS.alloc([128, 512], F32) for _ in range(2)]; g1_t = [Tile(f"g1_{i}") for i in range(2)]
        g2 = [RS.alloc([128, 512], F32) for _ in range(2)]; g2_t = [Tile(f"g2_{i}") for i in range(2)]
        gmf = [RS.alloc([128, 2, 128], F32) for _ in range(2)]; gmf_t = [Tile(f"gmf{i}") for i in range(2)]

        def fc_stages():
            lflat = lneg.rearrange("p a b -> p (a b)")
            zf, zf_t = g1[0][:, 0:128], g1_t[0]
            fe, fe_t = g1[1][:, 0:128], g1_t[1]
            fw, fw_t = g2[0][:, 0:128], g2_t[0]
            fp_, fp_t = g2[1][:, 0:128], g2_t[1]
            tot = fw.rearrange("p (a b) -> p a b", a=16)

            def fproj(t4):
                K.mm_multi(ps_t[6], [(ps[6][:, tb * 8:(tb + 1) * 8],
                                      [(hT[:, kc, tb * 128:(tb + 1) * 128], wf[:, kc, :]) for kc in range(16)])
                                     for tb in range(4 * t4, 4 * t4 + 4)], hT_t[4 * t4:4 * t4 + 4] + [wf_t])

            def sp():
                K.op(dve, lambda: V_.tensor_tensor(out=lflat, in0=ps[6][:, 0:128], in1=bfb, op=ALU.add),
                     [ps_t[6], bfb_t], [lneg_t])
                K.op(dve, lambda: V_.tensor_copy(out=zf, in_=lflat), [lneg_t], [zf_t])
                softplus_neg(zf, zf_t, lflat, lneg_t, fe, fe_t, fw, fw_t, fp_, fp_t)

            def totals():
                K.mm(ps_t[6], ps[6][:, 0:128], [(onesf, lflat)], [lneg_t, cst_t])
                K.op(dve, lambda: V_.memset(tot[:, 0, :], 0.0), [fw_t], [fw_t])
                K.op(dve, lambda: V_.tensor_copy(out=tot[:, 1, :], in_=ps[6][:, 0:8]), [ps_t[6]], [fw_t])
                for tb in range(2, 16):
                    K.op(dve, lambda: V_.tensor_tensor(out=tot[:, tb, :], in0=tot[:, tb - 1, :],
                                                       in1=ps[6][:, (tb - 1) * 8:tb * 8], op=ALU.add), [fw_t, ps_t[6]], [fw_t])

            def cums():
                K.mm(ps_t[7], ps[7][:, 0:128], [(triU, lflat)], [lneg_t, cst_t])
                K.op(dve, lambda: V_.tensor_tensor(out=negc.rearrange("p a b -> p (a b)"), in0=ps[7][:, 0:128], in1=fw,
                                                   op=ALU.add), [ps_t[7], fw_t], [negc_t])

            def crefs():
                K.mm_multi(ps_t[6], [(ps[6][:, qc * 8:(qc + 1) * 8], [(sel0, negc[:, 8 + 4 * qc, :])]) for qc in range(2)],
                           [negc_t, cst_t])
                K.op(dve, lambda: V_.tensor_copy(out=cref, in_=ps[6][:, 0:16]), [ps_t[6]], [cref_t])

            def biases():
                for qc in range(2):
                    crb = cref[:, qc * 8:(qc + 1) * 8].unsqueeze(1).to_broadcast([128, 16, 8])
                    K.op(dve, lambda: V_.tensor_tensor(out=biasall[:, qc, :, :], in0=negc, in1=crb, op=ALU.subtract),
                         [negc_t, cref_t], [bias_t])
                    K.op(dve, lambda: V_.tensor_scalar(out=biasall[:, qc, 0:8, :], in0=biasall[:, qc, 0:8, :],
                                                       scalar1=col(C_PMNEG), scalar2=None, op0=ALU.add),
                         [bias_t] + PVR, [bias_t])
                dump("negc", negc, [128, 16, 8], F32, [negc_t])

            return [lambda: fproj(0), lambda: fproj(1), lambda: fproj(2), lambda: fproj(3), sp, totals, cums, crefs, biases]

        def load_attn(h):
            wi = nxt("wbA", 2)
            wA = wbufA[wi].rearrange("p (a b) -> p a b", a=16)
            wt = wbufA_t[wi]
            for j3 in range(3):
                K.dma_in(pool, wt, wA[:, :, j3 * 128:(j3 + 1) * 128],
                         w_in_v[:, :, j3 * 1024 + h * 128:j3 * 1024 + (h + 1) * 128])
            return wA, wt

        def load_lru(n):
            wi = nxt("wbL", 2)
            wL = wbufL[wi].rearrange("p (a b) -> p a b", a=16)
            wt = wbufL_t[wi]
            K.dma_in(pool, wt, wL[:, :, 0:128], w_in_v[:, :, 3080 + n * 128:3080 + (n + 1) * 128])
            K.dma_in(pool, wt, wL[:, :, 128:256], w_in_v[:, :, 4104 + n * 128:4104 + (n + 1) * 128])
            gi = nxt("gm", 2)
            gm, gmt = gmf[gi], gmf_t[gi]
            K.dma_in(sp, gmt, gm[:, 0, :], w_ra[n, :, :])
            K.dma_in(sp, gmt, gm[:, 1, :], w_ri[n, :, :])
            return wL, wt, gm, gmt

        def attention_unit(h, ld, extra=None):
            wA, wt = ld
            extra = list(extra or [])

            def drain(n_):
                for _ in range(n_):
                    if extra:
                        extra.pop(0)()

            PB4 = (0, 1, 2, 3)
            hp = h % 2
            if KFOLD:
                hp = h % 2
                kst = {}

                def kproj(tc):
                    pb = PB4[nxt("pacc4", 4)]
                    K.mm(ps_t[pb], ps[pb], [(wA[:, kc, 128:256], hsl(kc, tc)) for kc in range(16)], [wt] + hts(tc))
                    k_ = nxt("hnbuf", 2)
                    sq, sq_t = HN["sq"][k_], HN["sq_t"][k_]
                    K.op(act, lambda: S_.activation(out=sq, in_=ps[pb], func=AF.Square), [ps_t[pb]], [sq_t])
                    kst[tc] = (pb, sq, sq_t)

                def kfin(tc):
                    pb, sq, sq_t = kst[tc]
                    K.mm_multi(ps_t[5], [(ps[5][:, (tc * 4 + b4) * 8:(tc * 4 + b4 + 1) * 8],
                                          [(sq[:, b4 * 128:(b4 + 1) * 128], onesb[:, 0:8])]) for b4 in range(4)], [sq_t, cst_t])
                    K.op(act, lambda: S_.activation(out=KT[:, tc * 512:(tc + 1) * 512], in_=ps[pb], func=AF.Copy,
                                                    scale=col(C_GK)), [ps_t[pb]] + PVR, [KT_t[tc]])
                    drain(2)

                kproj(0)
                for tc in range(4):
                    if tc + 1 < 4:
                        kproj(tc + 1)
                    kfin(tc)
                K.op(act, lambda: S_.activation(out=rsk[hp], in_=ps[5][:, 0:128], func=AF.Sqrt, scale=1.0 / 128, bias=col(C_EPS)),
                     [ps_t[5]] + PVR, [rsk_t[hp]])
                K.op(dve, lambda: V_.reciprocal(out=rsk[hp], in_=rsk[hp]), [rsk_t[hp]], [rsk_t[hp]])
            else:
                for tc in range(4):
                    pb = PB4[nxt("pacc4", 4)]
                    K.mm(ps_t[pb], ps[pb], [(wA[:, kc, 128:256], hsl(kc, tc)) for kc in range(16)], [wt] + hts(tc))
                    headnorm(pb, 512, KT_t[tc], KT[:, tc * 512:(tc + 1) * 512], C_GK, 1.0 / 128)
            qpb = []
            for qc in range(2):
                pb = PB4[nxt("pacc4", 4)]
                qpb.append(pb)
                K.mm(ps_t[pb], ps[pb], [(wA[:, kc, 0:128], hsl(kc, 2 + qc)) for kc in range(16)], [wt] + hts(2 + qc))
            for qc in range(2):
                headnorm(qpb[qc], 512, QT_t[qc], QT[:, qc * 512:(qc + 1) * 512], C_GQS, 1.0 / 128)
                drain(2)
            for g4 in range(4):
                pb = PB4[nxt("pacc4", 4)]
                for tbi in range(4):
                    tb = g4 * 4 + tbi
                    K.mm(ps_t[pb], ps[pb][:, tbi * 128:(tbi + 1) * 128],
                         [(hT[:, kc, tb * 128:(tb + 1) * 128], wA[:, kc, 256:384]) for kc in range(16)], [wt, hT_t[tb]])
                K.op(act, lambda: S_.copy(out=Vt[:, g4 * 4:(g4 + 1) * 4, :], in_=ps[pb].rearrange("p (a b) -> p a b", a=4)),
                     [ps_t[pb]], [V_t[g4]])
                drain(2)
            drain(len(extra))
            if h == 0:
                dump("KT0", KT, [128, NTOK], BF16, KT_t)
                dump("QT0", QT, [128, OWN], BF16, QT_t)
                dump("V0", Vt, [128, 16, 128], BF16, V_t)
            for qc in range(2):
                nkb = 8 + 4 * (qc + 1)
                bo, br = (4, 5) if qc == 0 else (6, 7)

                def smm(kb):
                    sb_ = 2 + (kb % 2)
                    prs = [(KT[:, kb * 128:(kb + 1) * 128], QT[:, qc * 512:(qc + 1) * 512])]
                    if kb >= 8 + 4 * qc:
                        prs.append((ident, masks[:, kb - 8 - 4 * qc, :]))
                    K.mm(ps_t[sb_], ps[sb_], prs, [KT_t[kb // 4], QT_t[qc], cst_t])

                smm(0)
                for kb in range(nkb):
                    sb_ = 2 + (kb % 2)
                    if kb + 1 < nkb:
                        smm(kb + 1)
                    pi_ = nxt("PT", 3)
                    K.op(act, lambda: S_.activation(out=PT[pi_], in_=ps[sb_], func=AF.Exp,
                                                    bias=biasall[:, qc, kb, h:h + 1],
                                                    scale=(rsk[hp][:, kb * 8:kb * 8 + 1] if (KFOLD and KSCALE) else 1.0)),
                         [ps_t[sb_], bias_t] + ([rsk_t[hp]] if (KFOLD and KSCALE) else []), [PT_t[pi_]])
                    K.mm(ps_t[bo], ps[bo], [(Vt[:, kb, :], PT[pi_])], [V_t[kb // 4], PT_t[pi_]],
                         start=(kb == 0), stop=(kb == nkb - 1))
                    K.mm(ps_t[br], ps[br], [(onesb, PT[pi_])], [PT_t[pi_], cst_t], start=(kb == 0), stop=(kb == nkb - 1))
                k_ = nxt("hnbuf", 2)
                rinv, rinv_t = HN["sr"][k_], HN["sr_t"][k_]
                K.op(dve, lambda: V_.reciprocal(out=rinv, in_=ps[br]), [ps_t[br]], [rinv_t])
                K.op(dve, lambda: V_.scalar_tensor_tensor(out=oT[:, h, qc * 512:(qc + 1) * 512], in0=ps[bo],
                                                          scalar=col(C_GFOX + h), in1=rinv, op0=ALU.mult, op1=ALU.mult),
                     [ps_t[bo], rinv_t] + PVR, [oT_t[h]])
            for st_ in extra:
                st_()

        def lru_unit(n, ld):
            wL, wt, gm, gmt = ld
            cw = lambda j: col(C_CW + j * 8 + n)
            GC = math.sqrt(2.0 / math.pi)
            L3 = {}

            def s1(tc):
                L = LS[tc % 2]
                u, ut = ur[tc % 2], ur_t[tc % 2]
                if tc == 0:
                    K.op(dve, lambda: V_.memset(u[:, 0:3], 0.0), [], [ut])
                else:
                    up = ur[(tc - 1) % 2]
                    K.op(dve, lambda: V_.tensor_copy(out=u[:, 0:3], in_=up[:, 512:515]), [ur_t[(tc - 1) % 2]], [ut])
                pb = 6 + (tc % 2)
                K.mm(ps_t[pb], ps[pb], [(wL[:, kc, 0:128], hsl(kc, tc)) for kc in range(16)], [wt] + hts(tc))
                K.op(act, lambda: S_.copy(out=u[:, 3:515], in_=ps[pb]), [ps_t[pb]], [ut])
                uc, uc_t = L["uc"], L["uc_t"]
                K.op(dve, lambda: V_.tensor_scalar(out=uc, in0=u[:, 3:515], scalar1=cw(3), scalar2=col(C_CB + n),
                                                   op0=ALU.mult, op1=ALU.add), [ut] + PVR, [uc_t])
                for j in range(3):
                    K.op(dve, lambda: V_.scalar_tensor_tensor(out=uc, in0=u[:, j:j + 512], scalar=cw(j), in1=uc,
                                                              op0=ALU.mult, op1=ALU.add), [ut, uc_t] + PVR, [uc_t])

            def s2a(tc):
                L = LS[tc % 2]
                uc, uc_t = L["uc"], L["uc_t"]
                K.mm(ps_t[7], ps[7], [(gm[:, 0, :], uc)], [gmt, uc_t])
                K.mm(ps_t[4], ps[4], [(gm[:, 1, :], uc)], [gmt, uc_t])
                K.op(act, lambda: S_.activation(out=L["thr"], in_=ps[7], func=AF.Tanh, bias=col(C_HBRA + n), scale=0.5),
                     [ps_t[7]] + PVR, [L["thr_t"]])
                K.op(act, lambda: S_.activation(out=L["thi"], in_=ps[4], func=AF.Tanh, bias=col(C_HBRI + n), scale=0.5),
                     [ps_t[4]] + PVR, [L["thi_t"]])
                K.op(act, lambda: S_.activation(out=L["aa"], in_=L["thr"], func=AF.Exp, bias=col(C_HSC1 + n),
                                                scale=col(C_HSC1 + n)), [L["thr_t"]] + PVR, [L["aa_t"]])
                K.op(act, lambda: S_.activation(out=L["tx"], in_=L["thr"], func=AF.Tanh, bias=col(C_NHSC1 + n),
                                                scale=col(C_NHSC1 + n)), [L["thr_t"]] + PVR, [L["tx_t"]])
                K.op(dve, lambda: V_.tensor_tensor(out=L["thr"], in0=L["aa"], in1=L["aa"], op=ALU.mult),
                     [L["aa_t"]], [L["thr_t"]])
                K.op(dve, lambda: V_.scalar_tensor_tensor(out=L["tx"], in0=L["thr"], scalar=1.0, in1=L["tx"], op0=ALU.add,
                                                          op1=ALU.mult), [L["thr_t"], L["tx_t"]], [L["tx_t"]])

            def s2sqrt(tc):
                L = LS[tc % 2]
                K.op(act, lambda: S_.activation(out=L["tx"], in_=L["tx"], func=AF.Sqrt), [L["tx_t"]], [L["tx_t"]])

            def s2b(tc):
                L = LS[tc % 2]
                K.op(dve, lambda: V_.scalar_tensor_tensor(out=L["thi"], in0=L["thi"], scalar=1.0, in1=L["uc"], op0=ALU.add,
                                                          op1=ALU.mult), [L["thi_t"], L["uc_t"]], [L["thi_t"]])
                sc_ = col(C_HPM) if tc < 2 else 0.5
                K.op(dve, lambda: V_.scalar_tensor_tensor(out=L["tx"], in0=L["thi"], scalar=sc_, in1=L["tx"], op0=ALU.mult,
                                                          op1=ALU.mult), [L["thi_t"], L["tx_t"]] + PVR, [L["tx_t"]])
                hcur, hct = hs[tc % 2], hs_t[tc % 2]
                if tc == 0:
                    K.op(dve, lambda: V_.tensor_tensor_scan(out=hcur, data0=L["aa"], data1=L["tx"], initial=0.0,
                                                            op0=ALU.mult, op1=ALU.add), [L["aa_t"], L["tx_t"]], [hct])
                else:
                    hp = hs[(tc - 1) % 2]
                    K.op(dve, lambda: V_.tensor_tensor_scan(out=hcur, data0=L["aa"], data1=L["tx"], initial=hp[:, 511:512],
                                                            op0=ALU.mult, op1=ALU.add),
                         [L["aa_t"], L["tx_t"], hs_t[(tc - 1) % 2]], [hct])

            def s3a(tc):
                k_ = tc % 2
                pb = 6 + k_
                K.mm(ps_t[pb], ps[pb], [(wL[:, kc, 128:256], hsl(kc, tc)) for kc in range(16)], [wt] + hts(tc))
                K.op(act, lambda: S_.activation(out=g1[k_], in_=ps[pb], func=AF.Square), [ps_t[pb]], [g1_t[k_]])
                K.op(dve, lambda: V_.tensor_scalar(out=g1[k_], in0=g1[k_], scalar1=0.044715, scalar2=1.0, op0=ALU.mult,
                                                   op1=ALU.add), [g1_t[k_]], [g1_t[k_]])
                K.op(dve, lambda: V_.tensor_tensor(out=g1[k_], in0=g1[k_], in1=ps[pb], op=ALU.mult),
                     [g1_t[k_], ps_t[pb]], [g1_t[k_]])
                K.op(act, lambda: S_.activation(out=g1[k_], in_=g1[k_], func=AF.Tanh, scale=GC), [g1_t[k_]], [g1_t[k_]])
                K.op(dve, lambda: V_.scalar_tensor_tensor(out=g2[k_], in0=g1[k_], scalar=1.0, in1=ps[pb], op0=ALU.add,
                                                          op1=ALU.mult), [g1_t[k_], ps_t[pb]], [g2_t[k_]])

            def s3b(tc):
                k_ = tc % 2
                K.op(dve, lambda: V_.scalar_tensor_tensor(out=yT[:, n, (tc - 2) * 512:(tc - 1) * 512], in0=g2[k_], scalar=col(C_HGLRU + n),
                                                          in1=hs[k_], op0=ALU.mult, op1=ALU.mult),
                     [g2_t[k_], hs_t[k_]] + PVR, [yT_t[n]])

            seq = [(s1, 0), (s1, 1), (s2a, 0), (s2a, 1), (s2sqrt, 0), (s2sqrt, 1), (s2b, 0), (s2b, 1),
                   (s1, 2), (s1, 3), (s3a, 2), (s2a, 2), (s2a, 3), (s3a, 3), (s2sqrt, 2), (s2sqrt, 3), (s2b, 2), (s3b, 2),
                   (s2b, 3), (s3b, 3)]
            return [(lambda f=f, a=a: f(a)) for f, a in seq]

        n_units = 8 if stop_after != "1b1" else 1
        ldA = {0: ldA0}
        ldL = {}
        pending = fc_stages()
        for p_ in range(n_units):
            if p_ + 1 < n_units:
                ldA[p_ + 1] = load_attn(p_ + 1)
            ldL[p_] = load_lru(p_)
            attention_unit(p_, ldA.pop(p_), extra=pending)
            pending = lru_unit(p_, ldL.pop(p_))
        def tok_sumsq(srcT, src_t, rg0, s_idx):
            for c in range(8):
                for tc in range(2):
                    si = nxt("hnbuf", 2)
                    sqb, sqb_t = HN["sq"][si], HN["sq_t"][si]
                    K.op(act, lambda: S_.activation(out=sqb, in_=srcT[:, c, tc * 512:(tc + 1) * 512], func=AF.Square,
                                                    scale=col(rg0 + c)), [src_t[c]] + PVR, [sqb_t])
                    K.mm_multi(ps_t[5], [(ps[5][:, (s_idx * 8 + tc * 4 + b4) * 8:(s_idx * 8 + tc * 4 + b4 + 1) * 8],
                                          [(sqb[:, b4 * 128:(b4 + 1) * 128], onesb[:, 0:8])]) for b4 in range(4)],
                               [sqb_t, cst_t], start=(c == 0 and tc == 0), stop=(c == 7), first_only=True)

        if n_units == 8:
            tok_sumsq(oT, oT_t, C_RGFOX, 0)
        for st_ in pending:
            st_()
        if n_units == 8:
            tok_sumsq(yT, yT_t, C_RGLRU, 1)
        dump("oT", oT, [128, 8, OWN], BF16, oT_t)
        dump("yT", yT, [128, 8, OWN], BF16, yT_t)

        def finish():
            deps = [(t.dsem, t.dcnt) for t in K.dma_tiles if t.dcnt > 0]
            deps += [(e.sem, e.cnt) for e in K.engs if e.cnt > 0]
            K._wait(sp, deps)

        if stop_after in ("1a", "1b", "1b1"):
            finish()
            return nc, dbg_out

        RH.reset(0)
        xtok = RH.alloc([128, 8, D], F32); xtok_t = [Tile(f"xtok{i}") for i in range(8)]

        def merge_r(dst, sv):
            k_ = id(sv[0])
            if k_ not in dst.r or dst.r[k_][1] < sv[1]:
                dst.r[k_] = sv

        for t_ in xtok_t:
            for ht in hT_t:
                for sv in ht.r.values():
                    merge_r(t_, sv)
                if ht.w is not None:
                    merge_r(t_, ht.w)
        wo_bufs = [(wbufA[0][:, 0:4096], wbufA_t[0]), (wbufL[0], wbufL_t[0]), (wbufA[1][:, 0:4096], wbufA_t[1]),
                   (wbufL[1], wbufL_t[1])]
        WOc = []
        for c8 in range(8):
            buf, bt = wo_bufs[c8 % 4]
            WOc.append((buf.rearrange("p (a b) -> p a b", a=16), bt))

        def load_wo(c8):
            K.dma_in(pool, WOc[c8][1], WOc[c8][0], w_out_v[:, :, c8 * 256:(c8 + 1) * 256])

        for c8 in range(4):
            load_wo(c8)
        for tb in range(8):
            K.dma_in(sp, xtok_t[tb], xtok[:, tb, :], xin[(8 + tb) * 128:(9 + tb) * 128, :])
        rs16, rs16_t = lneg.rearrange("p a b -> p (a b)"), lneg_t
        K.op(act, lambda: S_.activation(out=rs16, in_=ps[5][:, 0:128], func=AF.Sqrt, scale=1.0 / 1024, bias=col(C_EPS)),
             [ps_t[5]] + PVR, [rs16_t])
        K.op(dve, lambda: V_.reciprocal(out=rs16, in_=rs16), [rs16_t], [rs16_t])
        dump("rs16", rs16, [128, 128], F32, [rs16_t])

        for c8 in range(8):
            Wc, Wc_t = WOc[c8]
            for tb in range(8):
                pb = nxt("pacc4", 4)
                K.mm_multi(ps_t[pb], [(ps[pb][:, 0:256], [(oT[:, kc, tb * 128:(tb + 1) * 128], Wc[:, kc, :]) for kc in range(8)]),
                                      (ps[pb][:, 256:512], [(yT[:, kc, tb * 128:(tb + 1) * 128], Wc[:, 8 + kc, :]) for kc in range(8)])],
                           [Wc_t] + oT_t + yT_t)
                xs = xtok[:, tb, c8 * 256:(c8 + 1) * 256]
                K.op(dve, lambda: V_.scalar_tensor_tensor(out=xs, in0=ps[pb][:, 0:256], scalar=rs16[:, tb * 8:tb * 8 + 1], in1=xs,
                                                          op0=ALU.mult, op1=ALU.add), [ps_t[pb], xtok_t[tb], rs16_t], [xtok_t[tb]])
                K.op(dve, lambda: V_.scalar_tensor_tensor(out=xs, in0=ps[pb][:, 256:512], scalar=rs16[:, (8 + tb) * 8:(8 + tb) * 8 + 1], in1=xs,
                                                          op0=ALU.mult, op1=ALU.add), [ps_t[pb], xtok_t[tb], rs16_t], [xtok_t[tb]])
            if c8 + 4 < 8:
                load_wo(c8 + 4)
        dump("x1", xtok, [128, 8, D], F32, xtok_t)
        if stop_after == "1c":
            finish()
            return nc, dbg_out

        K.barrier()
        RX.reset(0)
        hT2 = RX.alloc([128, 16, OWN], BF16); hT2_t = [Tile(f"hT2_{i}") for i in range(8)]
        Wcq = RX.alloc([128, 16, 512], BF16); Wcq_t = Tile("Wcq")
        gbc = RX.alloc([128, D], F32); gbc_t = Tile("gbc2")
        hn = [RX.alloc([128, D], BF16) for _ in range(2)]; hn_t = [Tile(f"hn2_{i}") for i in range(2)]
        RS.reset(0)
        mnT = RS.alloc([128, 16, 256], BF16); mnT_t = [Tile(f"mnT{i}") for i in range(2)]
        Wck = RS.alloc([128, 16, 512], BF16); Wck_t = Tile("Wck")
        Wcv = RS.alloc([128, 16, 512], BF16); Wcv_t = Tile("Wcv")
        ckT = RS.alloc([128, 4, 256], BF16); ckT_t = Tile("ckT")
        cv = RS.alloc([128, 2, 512], BF16); cv_t = Tile("cv")
        cq_raw = RS.alloc([128, D], F32)
        ox_raw = RS.alloc([128, D], F32)
        cqT = cq_raw.bitcast(BF16).rearrange("p (a b) -> p a b", a=4); cqT_t = [Tile(f"cqT{i}") for i in range(4)]
        oxT = ox_raw.bitcast(BF16).rearrange("p (a b) -> p a b", a=4); oxT_t = [Tile(f"oxT{i}") for i in range(4)]
        PT = [RS.alloc([128, 512], BF16) for _ in range(3)]; PT_t = [Tile(f"PT2_{i}") for i in range(3)]
        HN["sq"] = [RS.alloc([128, 512], BF16) for _ in range(2)]; HN["sq_t"] = [Tile(f"sq2_{i}") for i in range(2)]
        HN["sr"] = [RS.alloc([128, 512], F32) for _ in range(2)]; HN["sr_t"] = [Tile(f"sr2_{i}") for i in range(2)]
        stat = [RS.alloc([128, 8], F32) for _ in range(2)]; stat_t = [Tile(f"stat2_{i}") for i in range(2)]

        K.dma_in(pool, Wck_t, Wck, w_ckv_v[:, :, 0:512])
        K.dma_in(pool, Wcv_t, Wcv, w_ckv_v[:, :, 512:1024])
        K.dma_in(pool, Wcq_t, Wcq, w_cq_v[:, :, :])
        gbc2 = gbc
        K.dma_in(sp, gbc_t, gbc2, grow[1:2, :].partition_broadcast(128))
        memst_t = [Tile("memst0"), Tile("gmem_bc")]
        K.dma_in(sp, memst_t[1], ox_raw, grow[2:3, :].partition_broadcast(128))
        K.dma_in(sp, memst_t[0], cq_raw, memb[0:128, :])
        def nr2(tb):
            i = tb % 2
            norm_rows(xtok_t[tb], xtok[:, tb, :], gbc_t, gbc2, hn[i], stat[i], stat_t[i], hn[i], hn_t[i])

        nr2(0)
        for tb in range(8):
            if tb + 1 < 8:
                nr2(tb + 1)
            transpose_rows(hn_t[tb % 2], hn[tb % 2], hT2, hT2_t[tb], tb * 128)

        for mb in range(2):
            if mb == 1:
                K.dma_in(sp, memst_t[0], cq_raw, memb[128:256, :])
            norm_rows(memst_t[0], cq_raw, memst_t[1], ox_raw, hn[mb], stat[mb], stat_t[mb], hn[mb], hn_t[mb])
            transpose_rows(hn_t[mb], hn[mb], mnT, mnT_t[mb], mb * 128)
        rsck = RS.alloc([128, 64], F32); rsck_t = Tile("rsck")
        for h in range(4):
            pb = nxt("pacc", 2)
            K.mm(ps_t[pb], ps[pb][:, 0:256], [(Wck[:, kc, h * 128:(h + 1) * 128], mnT[:, kc, :]) for kc in range(16)],
                 [Wck_t] + mnT_t)
            k_ = nxt("hnbuf", 2)
            sq, sq_t = HN["sq"][k_], HN["sq_t"][k_]
            K.op(act, lambda: S_.activation(out=sq[:, 0:256], in_=ps[pb][:, 0:256], func=AF.Square), [ps_t[pb]], [sq_t])
            K.mm_multi(ps_t[5], [(ps[5][:, (h * 2 + mb) * 8:(h * 2 + mb + 1) * 8],
                                  [(sq[:, mb * 128:(mb + 1) * 128], onesb[:, 0:8])]) for mb in range(2)], [sq_t, cst_t])
            K.op(act, lambda: S_.activation(out=ckT[:, h, :], in_=ps[pb][:, 0:256], func=AF.Copy, scale=col(C_GCK)),
                 [ps_t[pb]] + PVR, [ckT_t])
        K.op(act, lambda: S_.activation(out=rsck, in_=ps[5][:, 0:64], func=AF.Sqrt, scale=1.0 / 128, bias=col(C_EPS)),
             [ps_t[5]] + PVR, [rsck_t])
        K.op(dve, lambda: V_.reciprocal(out=rsck, in_=rsck), [rsck_t], [rsck_t])
        for mb in range(2):
            pb = nxt("pacc", 2)
            K.mm(ps_t[pb], ps[pb], [(mnT[:, kc, mb * 128:(mb + 1) * 128], Wcv[:, kc, :]) for kc in range(16)],
                 [Wcv_t, mnT_t[mb]])
            K.op(act, lambda: S_.copy(out=cv[:, mb, :], in_=ps[pb]), [ps_t[pb]], [cv_t])
        for t_ in cqT_t + oxT_t:
            for mt in memst_t:
                t_.r.update(mt.r)
                if mt.w is not None:
                    t_.r[id(mt.w[0])] = mt.w
        def hsl2(kc, tc):
            return hT2[:, kc, tc * 512:(tc + 1) * 512]

        def hts2(tc):
            return hT2_t[4 * tc:4 * tc + 4]

        cq_items = [(h, tc) for h in range(4) for tc in range(2)]
        cq_pb = {}

        def cq_proj(i_):
            h, tc = cq_items[i_]
            pb = nxt("pacc4", 4)
            cq_pb[i_] = pb
            K.mm(ps_t[pb], ps[pb], [(Wcq[:, kc, h * 128:(h + 1) * 128], hsl2(kc, tc)) for kc in range(16)],
                 [Wcq_t] + hts2(tc))

        cq_proj(0)
        cq_proj(1)
        for i_, (h, tc) in enumerate(cq_items):
            if i_ + 2 < len(cq_items):
                cq_proj(i_ + 2)
            headnorm(cq_pb[i_], 512, cqT_t[h], cqT[:, h, tc * 512:(tc + 1) * 512], C_GCQS, 1.0 / 128)
        def xs_mm(h, tc, mb, it):
            sb_ = 2 + (it % 2)
            K.mm(ps_t[sb_], ps[sb_], [(ckT[:, h, mb * 128:(mb + 1) * 128], cqT[:, h, tc * 512:(tc + 1) * 512])],
                 [ckT_t, cqT_t[h]])

        steps = [(h, tc, mb) for h in range(4) for tc in range(2) for mb in range(2)]
        xs_mm(*steps[0], 0)
        for it, (h, tc, mb) in enumerate(steps):
            sb_ = 2 + (it % 2)
            bo, br = (4, 5) if (it // 2) % 2 == 0 else (6, 7)
            if it + 1 < len(steps):
                xs_mm(*steps[it + 1], it + 1)
            pi_ = nxt("PT", 3)
            K.op(act, lambda: S_.activation(out=PT[pi_], in_=ps[sb_], func=AF.Exp,
                                            scale=rsck[:, (h * 2 + mb) * 8:(h * 2 + mb) * 8 + 1]),
                 [ps_t[sb_], rsck_t], [PT_t[pi_]])
            K.mm(ps_t[bo], ps[bo], [(cv[:, mb, h * 128:(h + 1) * 128], PT[pi_])], [cv_t, PT_t[pi_]],
                 start=(mb == 0), stop=(mb == 1))
            K.mm(ps_t[br], ps[br], [(onesb, PT[pi_])], [PT_t[pi_], cst_t], start=(mb == 0), stop=(mb == 1))
            if mb == 1:
                k_ = nxt("hnbuf", 2)
                rinv, rinv_t = HN["sr"][k_], HN["sr_t"][k_]
                K.op(dve, lambda: V_.reciprocal(out=rinv, in_=ps[br]), [ps_t[br]], [rinv_t])
                K.op(dve, lambda: V_.tensor_tensor(out=oxT[:, h, tc * 512:(tc + 1) * 512], in0=ps[bo], in1=rinv, op=ALU.mult),
                     [ps_t[bo], rinv_t], [oxT_t[h]])
        Wco = Wck.rearrange("p a b -> p (a b)")[:, 0:4 * D].rearrange("p (a b) -> p a b", a=4)
        K.dma_in(pool, Wck_t, Wco, w_co_v[:, :, :])
        for fc in range(4):
            for tb in range(8):
                pb = nxt("pacc", 2)
                K.mm(ps_t[pb], ps[pb], [(oxT[:, h, tb * 128:(tb + 1) * 128], Wco[:, h, fc * 512:(fc + 1) * 512])
                                        for h in range(4)], [Wck_t] + oxT_t)
                xs = xtok[:, tb, fc * 512:(fc + 1) * 512]
                K.op(dve, lambda: V_.tensor_tensor(out=xs, in0=ps[pb], in1=xs, op=ALU.add),
                     [ps_t[pb], xtok_t[tb]], [xtok_t[tb]])
        dump("x2", xtok, [128, 8, D], F32, xtok_t)
        if stop_after == "2":
            finish()
            return nc, dbg_out

        K.barrier()
        RX.reset(8192)
        actq = RX.alloc([128, 11, OWN], BF16); actq_t = [Tile(f"actq{i}") for i in range(11)]
        RS.reset(0)
        stat = [RS.alloc([128, 8], F32) for _ in range(2)]; stat_t = [Tile(f"stat3_{i}") for i in range(2)]
        Wgu = [RS.alloc([128, 16, 256], BF16) for _ in range(4)]; Wgu_t = [Tile(f"Wgu{i}") for i in range(4)]
        Wd = [RS.alloc([128, 11, 256], BF16) for _ in range(3)]; Wd_t = [Tile(f"Wd{i}") for i in range(3)]
        sg = [RS.alloc([128, 512], F32) for _ in range(2)]; sg_t = [Tile(f"sg{i}") for i in range(2)]
        gbc = RS.alloc([128, D], F32); gbc_t = Tile("gbc3")
        hn = [RS.alloc([128, D], BF16) for _ in range(2)]; hn_t = [Tile(f"hn3_{i}") for i in range(2)]
        hT2_t = [Tile(f"hT3_{i}") for i in range(8)]
        K.dma_in(sp, gbc_t, gbc, grow[3:4, :].partition_broadcast(128))
        hn4 = hn + [Wd[0].rearrange("p a b -> p (a b)")[:, 0:D], Wd[1].rearrange("p a b -> p (a b)")[:, 0:D]]
        hn4_t = hn_t + [Wd_t[0], Wd_t[1]]
        stat4 = [stat[0][:, 0:4], stat[0][:, 4:8], stat[1][:, 0:4], stat[1][:, 4:8]]
        stat4_t = [Tile(f"stat43_{i}") for i in range(4)]
        def nr3(tb):
            i = tb % 4
            norm_rows(xtok_t[tb], xtok[:, tb, :], gbc_t, gbc, hn4[i], stat4[i], stat4_t[i], hn4[i], hn4_t[i])

        for tb in range(2):
            nr3(tb)
        for tb in range(8):
            if tb + 2 < 8:
                nr3(tb + 2)
            transpose_rows(hn4_t[tb % 4], hn4[tb % 4], hT2, hT2_t[tb], tb * 128)
        for q in range(4):
            hc0 = q * 11
            hl = 0
            for gsz in (2, 2, 2, 2, 2, 1):
                gi_, ui_ = nxt("Wgu", 4), nxt("Wgu", 4)
                c0 = (hc0 + hl) * 128
                K.dma_in(pool, Wgu_t[gi_], Wgu[gi_][:, :, 0:gsz * 128], w_gu_v[:, :, c0:c0 + gsz * 128])
                K.dma_in(pool, Wgu_t[ui_], Wgu[ui_][:, :, 0:gsz * 128], w_gu_v[:, :, FFN + c0:FFN + c0 + gsz * 128])
                for g_ in range(gsz):
                    for tc in range(2):
                        pg, pu = tc, 2 + tc
                        K.mm(ps_t[pg], ps[pg], [(Wgu[gi_][:, kc, g_ * 128:(g_ + 1) * 128], hsl2(kc, tc)) for kc in range(16)],
                             [Wgu_t[gi_]] + hts2(tc))
                        K.mm(ps_t[pu], ps[pu], [(Wgu[ui_][:, kc, g_ * 128:(g_ + 1) * 128], hsl2(kc, tc)) for kc in range(16)],
                             [Wgu_t[ui_]] + hts2(tc))
                        si = nxt("sg", 2)
                        K.op(act, lambda: S_.activation(out=sg[si], in_=ps[pg], func=AF.Silu), [ps_t[pg]], [sg_t[si]])
                        K.op(dve, lambda: V_.tensor_tensor(out=actq[:, hl, tc * 512:(tc + 1) * 512], in0=sg[si], in1=ps[pu],
                                                           op=ALU.mult), [sg_t[si], ps_t[pu]], [actq_t[hl]])
                    hl += 1
            for f8 in range(8):
                di = nxt("Wd", 3)
                K.dma_in(pool, Wd_t[di], Wd[di], w_down_v[:, hc0:hc0 + 11, f8 * 256:(f8 + 1) * 256])
                for tb in range(8):
                    pd = 4 + nxt("pd", 4)
                    K.mm(ps_t[pd], ps[pd][:, 0:256],
                         [(actq[:, hl_, tb * 128:(tb + 1) * 128], Wd[di][:, hl_, :]) for hl_ in range(11)],
                         [Wd_t[di]] + actq_t)
                    xs = xtok[:, tb, f8 * 256:(f8 + 1) * 256]
                    K.op(dve, lambda: V_.tensor_tensor(out=xs, in0=ps[pd][:, 0:256], in1=xs, op=ALU.add),
                         [ps_t[pd], xtok_t[tb]], [xtok_t[tb]])
                if q == 3:
                    K._wait(sp, K._deps(sp, xtok_t, []))
                    st_t = Tile(f"ost{f8}")
                    K.dma_out(sp, st_t, out_v[:, :, f8 * 256:(f8 + 1) * 256], xtok[:, :, f8 * 256:(f8 + 1) * 256])
        finish()
    return nc, dbg_out


_NC_CACHE = {}


def _get_nc():
    return build()[0]


def make_in_maps(inp):
    f = lambda a: np.ascontiguousarray(np.asarray(a, dtype=np.float32))
    x, mem = f(inp["x"]), f(inp["mem"])
    pv = np.zeros((128, NV), np.float32)
    pv[:, C_GQ] = f(inp["g_q"])[0]
    pv[:, C_GK] = f(inp["g_k"])[0]
    pv[:, C_GCQ] = f(inp["g_cq"])[0]
    pv[:, C_GCK] = f(inp["g_ck"])[0]
    cwt = f(inp["conv_w"])[0]
    for j in range(4):
        pv[:, C_CW + j * 8:C_CW + j * 8 + 8] = cwt[j].reshape(8, 128).T
    for c0, key in ((C_CB, "conv_b"), (C_BRA, "b_ra"), (C_BRI, "b_ri"), (C_LAM, "lam"), (C_GFOX, "g_fox_out"),
                    (C_GLRU, "g_lru_out")):
        pv[:, c0:c0 + 8] = f(inp[key])[0].reshape(8, 128).T
    grow = np.stack([f(inp["g_mix"])[0], f(inp["g_xattn"])[0], f(inp["g_mem"])[0], f(inp["g_ffn"])[0]], 0)
    bfr = np.tile(f(inp["b_f"])[0], 16)[None, :]
    shared = {
        "grow": np.ascontiguousarray(grow), "bfr": np.ascontiguousarray(bfr),
        "w_in": f(inp["w_in"])[0], "w_ra": f(inp["w_ra"])[0], "w_ri": f(inp["w_ri"])[0], "w_out": f(inp["w_out"])[0],
        "w_cq": f(inp["w_cq"])[0], "w_ckv": f(inp["w_ckv"])[0], "w_co": f(inp["w_co"])[0],
        "w_gu": f(inp["w_gate_up"])[0], "w_down": f(inp["w_down"])[0],
    }
    maps = []
    for c in range(8):
        b, j = c // 2, c % 2
        if j == 1:
            xin = x[b]
        else:
            xin = np.zeros((NTOK, D), np.float32)
            xin[OWN:] = x[b, :OWN]
        pvc = pv.copy()
        pvc[:, C_PM] = float(j)
        m = dict(shared)
        m.update({"xin": np.ascontiguousarray(xin), "memb": np.ascontiguousarray(mem[b]), "pvec": pvc})
        maps.append(m)
    return maps


def kernel(**inputs):
    nc = _get_nc()
    maps = make_in_maps(inputs)
    res = run_bass_kernel_spmd(nc, maps, core_ids=list(range(8)))
    out = np.empty((4, 2048, D), np.float32)
    for c in range(8):
        b, j = c // 2, c % 2
        out[b, j * OWN:(j + 1) * OWN] = res.results[c]["out"]
    return out
```

```python
import math
from contextlib import ExitStack

import numpy as np
import concourse.bass as bass
import concourse.mybir as mybir
from concourse.bass_utils import run_bass_kernel_spmd

F32 = mybir.dt.float32
BF16 = mybir.dt.bfloat16
AF = mybir.ActivationFunctionType
ALU = mybir.AluOpType

D = 2048
NTOK = 2048
OWN = 1024
IN_W = 5128
FFN = 5632
NHC = 44
EPS = 1e-6
KFOLD = True
KSCALE = True

C_GQ, C_GK, C_GCQ, C_GCK, C_PM = 0, 1, 2, 3, 4
C_CW = 5
C_CB = 37
C_BRA = 45
C_BRI = 53
C_LAM = 61
C_GFOX = 69
C_GLRU = 77
NV = 85
C_GQS, C_GCQS, C_PMNEG, C_EPS, C_ONE = 85, 86, 87, 88, 89
C_SC1 = 90
C_TMP = 98
C_HBRA, C_HBRI, C_HSC1, C_NHSC1, C_HPM = 130, 138, 146, 154, 162
C_HGLRU, C_RGFOX, C_RGLRU = 168, 176, 184
PVW = 192


class Eng:
    def __init__(self, name, h, sem):
        self.name, self.h, self.sem, self.cnt, self.seen = name, h, sem, 0, {}


class Tile:
    __slots__ = ("name", "w", "r", "dsem", "dcnt")

    def __init__(self, name):
        self.name, self.w, self.r, self.dsem, self.dcnt = name, None, {}, None, 0


class KB:
    def __init__(self, nc, es):
        self.nc, self.es = nc, es
        mk = lambda n: es.enter_context(nc.semaphore(n))
        self.pe = Eng("pe", nc.tensor, mk("s_pe"))
        self.act = Eng("act", nc.scalar, mk("s_act"))
        self.dve = Eng("dve", nc.vector, mk("s_dve"))
        self.pool = Eng("pool", nc.gpsimd, mk("s_pool"))
        self.sp = Eng("sp", nc.sync, mk("s_sp"))
        self.engs = [self.pe, self.act, self.dve, self.pool, self.sp]
        self.free_sems = []
        self.dma_tiles = []
        self.nsem = 0

    def _wait(self, eng, deps):
        for sem, val in deps:
            k = id(sem)
            if eng.seen.get(k, 0) >= val:
                continue
            eng.h.wait_ge(sem, val)
            eng.seen[k] = val

    def _deps(self, eng, reads, writes, skip=None):
        d = {}

        def add(sv):
            if sv is None:
                return
            sem, val = sv
            if sem is skip:
                return
            k = id(sem)
            if k not in d or d[k][1] < val:
                d[k] = (sem, val)

        for t in reads:
            add(t.w)
        for t in writes:
            if t.w is not None and t.w[0] is not eng.sem:
                add(t.w)
            for sv in t.r.values():
                if sv[0] is not eng.sem:
                    add(sv)
        return list(d.values())

    def _mark(self, reads, writes, st):
        for t in reads:
            t.r[id(st[0])] = st
        for t in writes:
            t.w = st
            t.r = {}

    def op(self, eng, fn, reads=(), writes=()):
        self._wait(eng, self._deps(eng, reads, writes))
        ins = fn()
        eng.cnt += 1
        ins.then_inc(eng.sem, 1)
        self._mark(reads, writes, (eng.sem, eng.cnt))

    def mm(self, out_t, out_ap, pairs, reads, start=True, stop=True):
        eng = self.pe
        self._wait(eng, self._deps(eng, reads, [out_t]))
        n = len(pairs)
        ins = None
        for i, (l, r) in enumerate(pairs):
            ins = self.nc.tensor.matmul(out_ap, lhsT=l, rhs=r, start=(start and i == 0), stop=(stop and i == n - 1))
        eng.cnt += 1
        ins.then_inc(eng.sem, 1)
        self._mark(reads, [out_t], (eng.sem, eng.cnt))

    def mm_multi(self, out_t, segs, reads, start=True, stop=True, first_only=False):
        eng = self.pe
        self._wait(eng, self._deps(eng, reads, [out_t]))
        ins = None
        for si_, (out_ap, pairs) in enumerate(segs):
            n = len(pairs)
            for i, (l, r) in enumerate(pairs):
                st_ = start and i == 0 and (si_ == 0 or not first_only)
                ins = self.nc.tensor.matmul(out_ap, lhsT=l, rhs=r, start=st_, stop=(stop and i == n - 1))
        eng.cnt += 1
        ins.then_inc(eng.sem, 1)
        self._mark(reads, [out_t], (eng.sem, eng.cnt))

    def _getsem(self):
        if self.free_sems:
            return self.free_sems.pop()
        self.nsem += 1
        return (self.es.enter_context(self.nc.semaphore(f"s_d{self.nsem}")), 0)

    def dma_in(self, q, out_t, out_ap, in_ap):
        if out_t.dsem is None:
            out_t.dsem, out_t.dcnt = self._getsem()
            self.dma_tiles.append(out_t)
        self._wait(q, self._deps(q, [], [out_t], skip=out_t.dsem))
        ins = q.h.dma_start(out=out_ap, in_=in_ap)
        out_t.dcnt += 16
        ins.then_inc(out_t.dsem, 16)
        out_t.w = (out_t.dsem, out_t.dcnt)
        out_t.r = {}

    def dma_out(self, q, in_t, out_ap, in_ap):
        if in_t.dsem is None:
            in_t.dsem, in_t.dcnt = self._getsem()
            self.dma_tiles.append(in_t)
        self._wait(q, self._deps(q, [in_t], []))
        ins = q.h.dma_start(out=out_ap, in_=in_ap)
        in_t.dcnt += 16
        ins.then_inc(in_t.dsem, 16)
        in_t.r[id(in_t.dsem)] = (in_t.dsem, in_t.dcnt)

    def barrier(self):
        deps = [(e.sem, e.cnt) for e in self.engs if e.cnt > 0]
        deps += [(t.dsem, t.dcnt) for t in self.dma_tiles if t.dcnt > 0]
        for e in self.engs:
            self._wait(e, deps)


def build(debug=False, stop_after=None):
    nc = bass.Bass("TRN2", target_bir_lowering=False)

    def din(name, shape):
        return nc.dram_tensor(name, shape, F32, kind="ExternalInput").ap()

    xin = din("xin", [NTOK, D])
    memb = din("memb", [256, D])
    pvec = din("pvec", [128, NV])
    grow = din("grow", [4, D])
    bfr = din("bfr", [1, 128])
    w_in = din("w_in", [D, IN_W])
    w_ra = din("w_ra", [8, 128, 128])
    w_ri = din("w_ri", [8, 128, 128])
    w_out = din("w_out", [D, D])
    w_cq = din("w_cq", [D, 512])
    w_ckv = din("w_ckv", [D, 1024])
    w_co = din("w_co", [512, D])
    w_gu = din("w_gu", [D, 2 * FFN])
    w_down = din("w_down", [FFN, D])
    out = nc.dram_tensor("out", [OWN, D], F32, kind="ExternalOutput").ap()
    dbg_out = {}

    w_in_v = w_in.rearrange("(kc p) n -> p kc n", p=128)
    w_out_v = w_out.rearrange("(kc p) n -> p kc n", p=128)
    w_cq_v = w_cq.rearrange("(kc p) n -> p kc n", p=128)
    w_ckv_v = w_ckv.rearrange("(kc p) n -> p kc n", p=128)
    w_co_v = w_co.rearrange("(kc p) n -> p kc n", p=128)
    w_gu_v = w_gu.rearrange("(kc p) n -> p kc n", p=128)
    w_down_v = w_down.rearrange("(hc p) n -> p hc n", p=128)
    out_v = out.rearrange("(tb p) n -> p tb n", p=128)

    with ExitStack() as es:
        CONST_SZ, RX_SZ, RH_SZ, RS_SZ = 1856, 16384, 16384, 18500
        ARENA_F = CONST_SZ + RX_SZ + RH_SZ + RS_SZ
        arena = es.enter_context(nc.sbuf_tensor("arena", [128, ARENA_F], F32))
        psb = [es.enter_context(nc.psum_tensor(f"psb{i}", [128, 512], F32)) for i in range(8)]
        ps = [p[:, :] for p in psb]
        ps_t = [Tile(f"ps{i}") for i in range(8)]
        K = KB(nc, es)
        pe, act, dve, pool, sp = K.pe, K.act, K.dve, K.pool, K.sp
        V_, S_, G_ = nc.vector, nc.scalar, nc.gpsimd

        class Region:
            def __init__(self, base, size):
                self.base, self.size, self.off = base, size, 0

            def reset(self, to=0):
                self.off = to

            def alloc(self, shape, dt):
                n = int(np.prod(shape[1:]))
                nf = n if dt == F32 else (n + 1) // 2
                nf = (nf + 7) // 8 * 8
                assert self.off + nf <= self.size, ("SBUF region overflow", self.base, self.off, nf, self.size)
                a0 = self.base + self.off
                ap = arena[:, a0:a0 + nf]
                self.off += nf
                if dt != F32:
                    ap = ap.bitcast(dt)
                ap = ap[:, 0:n]
                if len(shape) == 3:
                    ap = ap.rearrange("p (a b) -> p a b", a=shape[1])
                elif len(shape) == 4:
                    ap = ap.rearrange("p (a b c) -> p a b c", a=shape[1], b=shape[2])
                return ap

        RC = Region(0, CONST_SZ)
        RX = Region(CONST_SZ, RX_SZ)
        RH = Region(CONST_SZ + RX_SZ, RH_SZ)
        RS = Region(CONST_SZ + RX_SZ + RH_SZ, RS_SZ)
        alloc = RC.alloc

        def bf(psap):
            return psap.bitcast(BF16)

        pv = alloc([128, PVW], F32); pv_t = Tile("pv")
        ident = alloc([128, 128], BF16)
        onesb = alloc([128, 128], BF16)
        onesf = alloc([128, 128], F32)
        triU = alloc([128, 128], F32)
        sel0 = alloc([128, 128], F32)
        masks = alloc([128, 4, 512], BF16)
        bfb = alloc([128, 128], F32)
        cst_t = Tile("consts")
        col = lambda c: pv[:, c:c + 1]

        K.dma_in(sp, pv_t, pv[:, 0:NV], pvec[:, :])
        bfb_t = Tile("bfb")
        K.dma_in(sp, bfb_t, bfb, bfr[0:1, :].partition_broadcast(128))
        K.op(pool, lambda: G_.memset(onesf, 1.0), [], [cst_t])
        K.op(pool, lambda: G_.memset(onesb, 1.0), [], [cst_t])
        K.op(pool, lambda: G_.memset(masks, 0.0), [], [cst_t])
        K.op(pool, lambda: G_.affine_select(out=ident, in_=onesb, pattern=[[-1, 128]], compare_op=ALU.is_equal,
                                            fill=0.0, base=0, channel_multiplier=1), [cst_t], [cst_t])
        K.op(pool, lambda: G_.affine_select(out=triU, in_=onesf, pattern=[[1, 128]], compare_op=ALU.is_ge,
                                            fill=0.0, base=0, channel_multiplier=-1), [cst_t], [cst_t])
        K.op(pool, lambda: G_.affine_select(out=sel0, in_=onesf, pattern=[[0, 128]], compare_op=ALU.is_ge,
                                            fill=0.0, base=0, channel_multiplier=-1), [cst_t], [cst_t])
        for r in range(4):
            K.op(pool, lambda r=r: G_.affine_select(out=masks[:, r, :], in_=masks[:, r, :], pattern=[[1, 512]],
                                                    compare_op=ALU.is_ge, fill=-30000.0, base=-128 * r,
                                                    channel_multiplier=-1), [cst_t], [cst_t])
        pv2_t = Tile("pv2")
        K.op(pool, lambda: G_.memset(col(C_EPS), EPS), [], [pv2_t])
        K.op(pool, lambda: G_.memset(col(C_ONE), 1.0), [], [pv2_t])
        sc = 1.0 / math.sqrt(128.0)
        K.op(dve, lambda: V_.tensor_scalar_mul(out=col(C_GQS), in0=col(C_GQ), scalar1=sc), [pv_t], [pv2_t])
        K.op(dve, lambda: V_.tensor_scalar_mul(out=col(C_GCQS), in0=col(C_GCQ), scalar1=sc), [pv_t], [pv2_t])
        K.op(dve, lambda: V_.tensor_scalar(out=col(C_PMNEG), in0=col(C_PM), scalar1=-1.0, scalar2=30000.0,
                                           op0=ALU.add, op1=ALU.mult), [pv_t], [pv2_t])
        def log1p_series(z, z_t, out_, out_t, w, w_t, p, p_t):
            K.op(dve, lambda: V_.tensor_scalar_add(out=w, in0=z, scalar1=2.0), [z_t], [w_t])
            K.op(dve, lambda: V_.reciprocal(out=w, in_=w), [w_t], [w_t])
            K.op(dve, lambda: V_.tensor_tensor(out=w, in0=w, in1=z, op=ALU.mult), [w_t, z_t], [w_t])
            K.op(dve, lambda: V_.tensor_tensor(out=out_, in0=w, in1=w, op=ALU.mult), [w_t], [out_t])
            K.op(dve, lambda: V_.tensor_scalar_mul(out=p, in0=out_, scalar1=1.0 / 9), [out_t], [p_t])
            for c_ in (1.0 / 7, 1.0 / 5, 1.0 / 3):
                K.op(dve, lambda: V_.scalar_tensor_tensor(out=p, in0=p, scalar=c_, in1=out_, op0=ALU.add, op1=ALU.mult),
                     [p_t, out_t], [p_t])
            K.op(dve, lambda: V_.scalar_tensor_tensor(out=p, in0=p, scalar=1.0, in1=w, op0=ALU.add, op1=ALU.mult),
                 [p_t, w_t], [p_t])
            K.op(dve, lambda: V_.tensor_scalar_mul(out=out_, in0=p, scalar1=2.0), [p_t], [out_t])

        def softplus_neg(x, x_t, out_, out_t, e, e_t, w, w_t, p, p_t):
            K.op(act, lambda: S_.activation(out=e, in_=x, func=AF.Abs), [x_t], [e_t])
            K.op(act, lambda: S_.activation(out=e, in_=e, func=AF.Exp, scale=-1.0), [e_t], [e_t])
            log1p_series(e, e_t, out_, out_t, w, w_t, p, p_t)
            K.op(dve, lambda: V_.tensor_scalar(out=w, in0=x, scalar1=-1.0, scalar2=0.0, op0=ALU.mult, op1=ALU.max),
                 [x_t], [w_t])
            K.op(dve, lambda: V_.tensor_tensor(out=out_, in0=out_, in1=w, op=ALU.add), [out_t, w_t], [out_t])

        tA, tB, tC, tD = (pv[:, C_TMP + 8 * i:C_TMP + 8 * i + 8] for i in range(4))
        tA_t, tB_t, tC_t, tD_t = Tile("tA"), Tile("tB"), Tile("tC"), Tile("tD")
        softplus_neg(pv[:, C_LAM:C_LAM + 8], pv_t, tA, tA_t, tB, tB_t, tC, tC_t, tD, tD_t)
        K.op(dve, lambda: V_.tensor_scalar_mul(out=pv[:, C_SC1:C_SC1 + 8], in0=tA, scalar1=-8.0), [tA_t], [pv2_t])
        K.op(dve, lambda: V_.tensor_scalar_mul(out=pv[:, C_HSC1:C_HSC1 + 8], in0=tA, scalar1=-4.0), [tA_t], [pv2_t])
        K.op(dve, lambda: V_.tensor_scalar_mul(out=pv[:, C_NHSC1:C_NHSC1 + 8], in0=tA, scalar1=4.0), [tA_t], [pv2_t])
        K.op(dve, lambda: V_.tensor_scalar_mul(out=pv[:, C_HBRA:C_HBRA + 8], in0=pv[:, C_BRA:C_BRA + 8], scalar1=0.5),
             [pv_t], [pv2_t])
        K.op(dve, lambda: V_.tensor_scalar_mul(out=pv[:, C_HBRI:C_HBRI + 8], in0=pv[:, C_BRI:C_BRI + 8], scalar1=0.5),
             [pv_t], [pv2_t])
        K.op(dve, lambda: V_.tensor_scalar_mul(out=col(C_HPM), in0=col(C_PM), scalar1=0.5), [pv_t], [pv2_t])
        K.op(dve, lambda: V_.tensor_scalar_mul(out=pv[:, C_HGLRU:C_HGLRU + 8], in0=pv[:, C_GLRU:C_GLRU + 8], scalar1=0.5),
             [pv_t], [pv2_t])
        K.op(dve, lambda: V_.reciprocal(out=pv[:, C_RGFOX:C_RGFOX + 16], in_=pv[:, C_GFOX:C_GFOX + 16]), [pv_t], [pv2_t])
        PVR = [pv_t, pv2_t]

        rot = {}

        def nxt(key, n):
            rot[key] = (rot.get(key, -1) + 1) % n
            return rot[key]

        def norm_rows(src_t, src_ap, gbc_t, gbc, junk, stat, stat_t, hn, hn_t):
            K.op(act, lambda: S_.activation(out=junk, in_=src_ap, func=AF.Square, accum_out=stat[:, 0:1]),
                 [src_t], [stat_t, hn_t])
            K.op(act, lambda: S_.activation(out=stat[:, 1:2], in_=stat[:, 0:1], func=AF.Sqrt, scale=1.0 / D,
                                            bias=col(C_EPS)), [stat_t] + PVR, [stat_t])
            K.op(dve, lambda: V_.reciprocal(out=stat[:, 2:3], in_=stat[:, 1:2]), [stat_t], [stat_t])
            K.op(dve, lambda: V_.scalar_tensor_tensor(out=hn, in0=src_ap, scalar=stat[:, 2:3], in1=gbc,
                                                      op0=ALU.mult, op1=ALU.mult), [src_t, stat_t, gbc_t], [hn_t])

        def transpose_rows(hn_t, hn, dstT, dst_t, c0):
            bp = 4 + 2 * nxt("trbank", 2)
            for half in range(2):
                b = bp + half
                pt = bf(ps[b])
                K._wait(pe, K._deps(pe, [hn_t, cst_t], [ps_t[b]]))
                ins = None
                for k8 in range(8):
                    kc = half * 8 + k8
                    ins = nc.tensor.transpose(pt[:, k8 * 128:(k8 + 1) * 128], hn[:, kc * 128:(kc + 1) * 128], ident)
                pe.cnt += 1
                ins.then_inc(pe.sem, 1)
                K._mark([hn_t, cst_t], [ps_t[b]], (pe.sem, pe.cnt))
                src = pt.rearrange("p (a b) -> p a b", a=8)
                dst = dstT[:, half * 8:(half + 1) * 8, c0:c0 + 128]
                if half == 0:
                    K.op(act, lambda: S_.copy(out=dst, in_=src), [ps_t[b]], [dst_t])
                else:
                    K.op(dve, lambda: V_.tensor_copy(out=dst, in_=src), [ps_t[b]], [dst_t])

        HN = {}

        def headnorm(pb, n, dst_t, dst_ap, gcol, inv_d):
            k_ = nxt("hnbuf", 2)
            sq, sq_t, sr, sr_t = HN["sq"][k_], HN["sq_t"][k_], HN["sr"][k_], HN["sr_t"][k_]
            K.op(act, lambda: S_.activation(out=sq[:, 0:n], in_=ps[pb][:, 0:n], func=AF.Square), [ps_t[pb]], [sq_t])
            K.mm(ps_t[5], ps[5][:, 0:n], [(onesb, sq[:, 0:n])], [sq_t, cst_t])
            K.op(act, lambda: S_.activation(out=sr[:, 0:n], in_=ps[5][:, 0:n], func=AF.Sqrt, scale=inv_d,
                                            bias=col(C_EPS)), [ps_t[5]] + PVR, [sr_t])
            K.op(dve, lambda: V_.reciprocal(out=sr[:, 0:n], in_=sr[:, 0:n]), [sr_t], [sr_t])
            K.op(dve, lambda: V_.scalar_tensor_tensor(out=dst_ap, in0=ps[pb][:, 0:n], scalar=col(gcol), in1=sr[:, 0:n],
                                                      op0=ALU.mult, op1=ALU.mult), [ps_t[pb], sr_t] + PVR, [dst_t])

        def dump(name, ap, shape, dt, tiles):
            if not debug:
                return
            d = nc.dram_tensor("dbg_" + name, shape, dt, kind="ExternalOutput").ap()
            dbg_out[name] = (shape, dt)
            t = Tile("dbgsrc_" + name)
            K._wait(sp, K._deps(sp, tiles, []))
            K.dma_out(sp, t, d, ap)
            for tt in tiles:
                tt.r[id(t.dsem)] = (t.dsem, t.dcnt)

        hT = RH.alloc([128, 16, NTOK], BF16); hT_t = [Tile(f"hT{i}") for i in range(16)]
        gbc = RX.alloc([128, D], F32); gbc_t = Tile("gbc")
        xst = [RX.alloc([128, D], F32) for _ in range(2)]; xst_t = [Tile(f"xst{i}") for i in range(2)]
        hn = [RX.alloc([128, D], BF16) for _ in range(2)]; hn_t = [Tile(f"hn{i}") for i in range(2)]
        assert RX.off <= 8192
        RX.reset(8192)
        oT = RX.alloc([128, 8, OWN], BF16); oT_t = [Tile(f"oT{i}") for i in range(8)]
        yT = RX.alloc([128, 8, OWN], BF16); yT_t = [Tile(f"yT{i}") for i in range(8)]
        wbufA = [RS.alloc([128, 16 * 384], BF16) for _ in range(2)]; wbufA_t = [Tile(f"wbA{i}") for i in range(2)]
        wbufL = [RS.alloc([128, 16 * 256], BF16) for _ in range(2)]; wbufL_t = [Tile(f"wbL{i}") for i in range(2)]
        KT = RS.alloc([128, NTOK], BF16); KT_t = [Tile(f"KT{i}") for i in range(4)]
        Vt = RS.alloc([128, 16, 128], BF16); V_t = [Tile(f"V{i}") for i in range(4)]
        QT = RS.alloc([128, OWN], BF16); QT_t = [Tile(f"QT{i}") for i in range(2)]
        PT = [RS.alloc([128, 512], BF16) for _ in range(3)]; PT_t = [Tile(f"PT{i}") for i in range(3)]
        HN["sq"] = [RS.alloc([128, 512], BF16) for _ in range(2)]; HN["sq_t"] = [Tile(f"sq{i}") for i in range(2)]
        HN["sr"] = [RS.alloc([128, 512], F32) for _ in range(2)]; HN["sr_t"] = [Tile(f"sr{i}") for i in range(2)]
        wf = RS.alloc([128, 16, 8], BF16); wf_t = Tile("wf")
        lneg = RS.alloc([128, 16, 8], F32); lneg_t = Tile("lneg")
        negc = RS.alloc([128, 16, 8], F32); negc_t = Tile("negc")
        cref = RS.alloc([128, 16], F32); cref_t = Tile("cref")
        biasall = RS.alloc([128, 2, 16, 8], F32); bias_t = Tile("biasall")
        stat = [RS.alloc([128, 8], F32) for _ in range(2)]; stat_t = [Tile(f"stat{i}") for i in range(2)]

        xst = xst + [wbufL[0].bitcast(F32), wbufL[1].bitcast(F32)]
        xst_t = xst_t + [Tile("xst2"), Tile("xst3")]
        hn = hn + [KT[:, 0:D], Vt.rearrange("p a b -> p (a b)")]
        hn_t = hn_t + [Tile("hn2"), Tile("hn3")]
        stat4 = [stat[0][:, 0:4], stat[0][:, 4:8], stat[1][:, 0:4], stat[1][:, 4:8]]
        stat4_t = [Tile(f"stat4_{i}") for i in range(4)]
        K.dma_in(sp, gbc_t, gbc, grow[0:1, :].partition_broadcast(128))
        K.dma_in(pool, wf_t, wf, w_in_v[:, :, 3072:3080])
        def nr1a(tb):
            i = tb % 4
            K.dma_in(sp, xst_t[i], xst[i], xin[tb * 128:(tb + 1) * 128, :])
            norm_rows(xst_t[i], xst[i], gbc_t, gbc, hn[i], stat4[i], stat4_t[i], hn[i], hn_t[i])

        SK = 2
        for tb in range(SK):
            nr1a(tb)
        for tb in range(16):
            if tb + SK < 16:
                nr1a(tb + SK)
            transpose_rows(hn_t[tb % 4], hn[tb % 4], hT, hT_t[tb], tb * 128)
        xst, xst_t, hn, hn_t = xst[:2], xst_t[:2], hn[:2], hn_t[:2]
        dump("hT", hT, [128, 16, NTOK], BF16, hT_t)

        def hsl(kc, tc):
            return hT[:, kc, tc * 512:(tc + 1) * 512]

        def hts(tc):
            return hT_t[4 * tc:4 * tc + 4]

        _wA0 = wbufA[0].rearrange("p (a b) -> p a b", a=16)
        for j3 in range(3):
            K.dma_in(pool, wbufA_t[0], _wA0[:, :, j3 * 128:(j3 + 1) * 128], w_in_v[:, :, j3 * 1024:j3 * 1024 + 128])
        rot["wbA"] = 0
        ldA0 = (_wA0, wbufA_t[0])

        K.barrier()
        RX.reset(0)
        ur = [RX.alloc([128, 520], F32) for _ in range(2)]; ur_t = [Tile(f"ur{i}") for i in range(2)]
        LS = []
        for k_ in range(2):
            d_ = {}
            for nm in ("uc", "thr", "thi", "aa", "tx"):
                d_[nm] = RX.alloc([128, 512], F32)
                d_[nm + "_t"] = Tile(f"{nm}{k_}")
            LS.append(d_)
        hs = [RX.alloc([128, 512], F32) for _ in range(2)]; hs_t = [Tile(f"hs{i}") for i in range(2)]
        rsk = [RX.alloc([128, 128], F32) for _ in range(2)]; rsk_t = [Tile(f"rsk{i}") for i in range(2)]
        assert RX.off <= 8192, RX.off
        g1 = [RS.alloc([128, 512], F32) for _ in range(2)]; g1_t = [Tile(f"g1_{i}") for i in range(2)]
        g2 = [RS.alloc([128, 512], F32) for _ in range(2)]; g2_t = [Tile(f"g2_{i}") for i in range(2)]
        gmf = [RS.alloc([128, 2, 128], F32) for _ in range(2)]; gmf_t = [Tile(f"gmf{i}") for i in range(2)]

        def fc_stages():
            lflat = lneg.rearrange("p a b -> p (a b)")
            zf, zf_t = g1[0][:, 0:128], g1_t[0]
            fe, fe_t = g1[1][:, 0:128], g1_t[1]
            fw, fw_t = g2[0][:, 0:128], g2_t[0]
            fp_, fp_t = g2[1][:, 0:128], g2_t[1]
            tot = fw.rearrange("p (a b) -> p a b", a=16)

            def fproj(t4):
                K.mm_multi(ps_t[6], [(ps[6][:, tb * 8:(tb + 1) * 8],
                                      [(hT[:, kc, tb * 128:(tb + 1) * 128], wf[:, kc, :]) for kc in range(16)])
                                     for tb in range(4 * t4, 4 * t4 + 4)], hT_t[4 * t4:4 * t4 + 4] + [wf_t])

            def sp():
                K.op(dve, lambda: V_.tensor_tensor(out=lflat, in0=ps[6][:, 0:128], in1=bfb, op=ALU.add),
                     [ps_t[6], bfb_t], [lneg_t])
                K.op(dve, lambda: V_.tensor_copy(out=zf, in_=lflat), [lneg_t], [zf_t])
                softplus_neg(zf, zf_t, lflat, lneg_t, fe, fe_t, fw, fw_t, fp_, fp_t)

            def totals():
                K.mm(ps_t[6], ps[6][:, 0:128], [(onesf, lflat)], [lneg_t, cst_t])
                K.op(dve, lambda: V_.memset(tot[:, 0, :], 0.0), [fw_t], [fw_t])
                K.op(dve, lambda: V_.tensor_copy(out=tot[:, 1, :], in_=ps[6][:, 0:8]), [ps_t[6]], [fw_t])
                for tb in range(2, 16):
                    K.op(dve, lambda: V_.tensor_tensor(out=tot[:, tb, :], in0=tot[:, tb - 1, :],
                                                       in1=ps[6][:, (tb - 1) * 8:tb * 8], op=ALU.add), [fw_t, ps_t[6]], [fw_t])

            def cums():
                K.mm(ps_t[7], ps[7][:, 0:128], [(triU, lflat)], [lneg_t, cst_t])
                K.op(dve, lambda: V_.tensor_tensor(out=negc.rearrange("p a b -> p (a b)"), in0=ps[7][:, 0:128], in1=fw,
                                                   op=ALU.add), [ps_t[7], fw_t], [negc_t])

            def crefs():
                K.mm_multi(ps_t[6], [(ps[6][:, qc * 8:(qc + 1) * 8], [(sel0, negc[:, 8 + 4 * qc, :])]) for qc in range(2)],
                           [negc_t, cst_t])
                K.op(dve, lambda: V_.tensor_copy(out=cref, in_=ps[6][:, 0:16]), [ps_t[6]], [cref_t])

            def biases():
                for qc in range(2):
                    crb = cref[:, qc * 8:(qc + 1) * 8].unsqueeze(1).to_broadcast([128, 16, 8])
                    K.op(dve, lambda: V_.tensor_tensor(out=biasall[:, qc, :, :], in0=negc, in1=crb, op=ALU.subtract),
                         [negc_t, cref_t], [bias_t])
                    K.op(dve, lambda: V_.tensor_scalar(out=biasall[:, qc, 0:8, :], in0=biasall[:, qc, 0:8, :],
                                                       scalar1=col(C_PMNEG), scalar2=None, op0=ALU.add),
                         [bias_t] + PVR, [bias_t])
                dump("negc", negc, [128, 16, 8], F32, [negc_t])

            return [lambda: fproj(0), lambda: fproj(1), lambda: fproj(2), lambda: fproj(3), sp, totals, cums, crefs, biases]

        def load_attn(h):
            wi = nxt("wbA", 2)
            wA = wbufA[wi].rearrange("p (a b) -> p a b", a=16)
            wt = wbufA_t[wi]
            for j3 in range(3):
                K.dma_in(pool, wt, wA[:, :, j3 * 128:(j3 + 1) * 128],
                         w_in_v[:, :, j3 * 1024 + h * 128:j3 * 1024 + (h + 1) * 128])
            return wA, wt

        def load_lru(n):
            wi = nxt("wbL", 2)
            wL = wbufL[wi].rearrange("p (a b) -> p a b", a=16)
            wt = wbufL_t[wi]
            K.dma_in(pool, wt, wL[:, :, 0:128], w_in_v[:, :, 3080 + n * 128:3080 + (n + 1) * 128])
            K.dma_in(pool, wt, wL[:, :, 128:256], w_in_v[:, :, 4104 + n * 128:4104 + (n + 1) * 128])
            gi = nxt("gm", 2)
            gm, gmt = gmf[gi], gmf_t[gi]
            K.dma_in(sp, gmt, gm[:, 0, :], w_ra[n, :, :])
            K.dma_in(sp, gmt, gm[:, 1, :], w_ri[n, :, :])
            return wL, wt, gm, gmt

        def attention_unit(h, ld, extra=None):
            wA, wt = ld
            extra = list(extra or [])
            per = -(-len(extra) // 10)

            def drain(n_):
                for _ in range(n_):
                    if extra:
                        extra.pop(0)()

            PB4 = (0, 1, 2, 3)
            hp = h % 2
            if KFOLD:
                hp = h % 2
                kst = {}

                def kproj(tc):
                    pb = PB4[nxt("pacc4", 4)]
                    K.mm(ps_t[pb], ps[pb], [(wA[:, kc, 128:256], hsl(kc, tc)) for kc in range(16)], [wt] + hts(tc))
                    k_ = nxt("hnbuf", 2)
                    sq, sq_t = HN["sq"][k_], HN["sq_t"][k_]
                    K.op(act, lambda: S_.activation(out=sq, in_=ps[pb], func=AF.Square), [ps_t[pb]], [sq_t])
                    kst[tc] = (pb, sq, sq_t)

                def kfin(tc):
                    pb, sq, sq_t = kst[tc]
                    K.mm_multi(ps_t[5], [(ps[5][:, (tc * 4 + b4) * 8:(tc * 4 + b4 + 1) * 8],
                                          [(sq[:, b4 * 128:(b4 + 1) * 128], onesb[:, 0:8])]) for b4 in range(4)], [sq_t, cst_t])
                    K.op(act, lambda: S_.activation(out=KT[:, tc * 512:(tc + 1) * 512], in_=ps[pb], func=AF.Copy,
                                                    scale=col(C_GK)), [ps_t[pb]] + PVR, [KT_t[tc]])
                    drain(per)

                kproj(0)
                for tc in range(4):
                    if tc + 1 < 4:
                        kproj(tc + 1)
                    kfin(tc)
                K.op(act, lambda: S_.activation(out=rsk[hp], in_=ps[5][:, 0:128], func=AF.Sqrt, scale=1.0 / 128, bias=col(C_EPS)),
                     [ps_t[5]] + PVR, [rsk_t[hp]])
                K.op(dve, lambda: V_.reciprocal(out=rsk[hp], in_=rsk[hp]), [rsk_t[hp]], [rsk_t[hp]])
            else:
                for tc in range(4):
                    pb = PB4[nxt("pacc4", 4)]
                    K.mm(ps_t[pb], ps[pb], [(wA[:, kc, 128:256], hsl(kc, tc)) for kc in range(16)], [wt] + hts(tc))
                    headnorm(pb, 512, KT_t[tc], KT[:, tc * 512:(tc + 1) * 512], C_GK, 1.0 / 128)
            qpb = []
            for qc in range(2):
                pb = PB4[nxt("pacc4", 4)]
                qpb.append(pb)
                K.mm(ps_t[pb], ps[pb], [(wA[:, kc, 0:128], hsl(kc, 2 + qc)) for kc in range(16)], [wt] + hts(2 + qc))
            for qc in range(2):
                headnorm(qpb[qc], 512, QT_t[qc], QT[:, qc * 512:(qc + 1) * 512], C_GQS, 1.0 / 128)
                drain(per)
            for g4 in range(4):
                pb = PB4[nxt("pacc4", 4)]
                for tbi in range(4):
                    tb = g4 * 4 + tbi
                    K.mm(ps_t[pb], ps[pb][:, tbi * 128:(tbi + 1) * 128],
                         [(hT[:, kc, tb * 128:(tb + 1) * 128], wA[:, kc, 256:384]) for kc in range(16)], [wt, hT_t[tb]])
                K.op(act, lambda: S_.copy(out=Vt[:, g4 * 4:(g4 + 1) * 4, :], in_=ps[pb].rearrange("p (a b) -> p a b", a=4)),
                     [ps_t[pb]], [V_t[g4]])
                drain(per)
            drain(len(extra))
            if h == 0:
                dump("KT0", KT, [128, NTOK], BF16, KT_t)
                dump("QT0", QT, [128, OWN], BF16, QT_t)
                dump("V0", Vt, [128, 16, 128], BF16, V_t)
            for qc in range(2):
                nkb = 8 + 4 * (qc + 1)
                bo, br = (4, 5) if qc == 0 else (6, 7)

                def smm(kb):
                    sb_ = 2 + (kb % 2)
                    prs = [(KT[:, kb * 128:(kb + 1) * 128], QT[:, qc * 512:(qc + 1) * 512])]
                    if kb >= 8 + 4 * qc:
                        prs.append((ident, masks[:, kb - 8 - 4 * qc, :]))
                    K.mm(ps_t[sb_], ps[sb_], prs, [KT_t[kb // 4], QT_t[qc], cst_t])

                smm(0)
                for kb in range(nkb):
                    sb_ = 2 + (kb % 2)
                    if kb + 1 < nkb:
                        smm(kb + 1)
                    pi_ = nxt("PT", 3)
                    K.op(act, lambda: S_.activation(out=PT[pi_], in_=ps[sb_], func=AF.Exp,
                                                    bias=biasall[:, qc, kb, h:h + 1],
                                                    scale=(rsk[hp][:, kb * 8:kb * 8 + 1] if (KFOLD and KSCALE) else 1.0)),
                         [ps_t[sb_], bias_t] + ([rsk_t[hp]] if (KFOLD and KSCALE) else []), [PT_t[pi_]])
                    K.mm(ps_t[bo], ps[bo], [(Vt[:, kb, :], PT[pi_])], [V_t[kb // 4], PT_t[pi_]],
                         start=(kb == 0), stop=(kb == nkb - 1))
                    K.mm(ps_t[br], ps[br], [(onesb, PT[pi_])], [PT_t[pi_], cst_t], start=(kb == 0), stop=(kb == nkb - 1))
                k_ = nxt("hnbuf", 2)
                rinv, rinv_t = HN["sr"][k_], HN["sr_t"][k_]
                K.op(dve, lambda: V_.reciprocal(out=rinv, in_=ps[br]), [ps_t[br]], [rinv_t])
                K.op(dve, lambda: V_.scalar_tensor_tensor(out=oT[:, h, qc * 512:(qc + 1) * 512], in0=ps[bo],
                                                          scalar=col(C_GFOX + h), in1=rinv, op0=ALU.mult, op1=ALU.mult),
                     [ps_t[bo], rinv_t] + PVR, [oT_t[h]])
            for st_ in extra:
                st_()

        def lru_unit(n, ld):
            wL, wt, gm, gmt = ld
            cw = lambda j: col(C_CW + j * 8 + n)
            GC = math.sqrt(2.0 / math.pi)
            L3 = {}

            def s1(tc):
                L = LS[tc % 2]
                u, ut = ur[tc % 2], ur_t[tc % 2]
                if tc == 0:
                    K.op(dve, lambda: V_.memset(u[:, 0:3], 0.0), [], [ut])
                else:
                    up = ur[(tc - 1) % 2]
                    K.op(dve, lambda: V_.tensor_copy(out=u[:, 0:3], in_=up[:, 512:515]), [ur_t[(tc - 1) % 2]], [ut])
                pb = 6 + (tc % 2)
                K.mm(ps_t[pb], ps[pb], [(wL[:, kc, 0:128], hsl(kc, tc)) for kc in range(16)], [wt] + hts(tc))
                K.op(act, lambda: S_.copy(out=u[:, 3:515], in_=ps[pb]), [ps_t[pb]], [ut])
                uc, uc_t = L["uc"], L["uc_t"]
                K.op(dve, lambda: V_.tensor_scalar(out=uc, in0=u[:, 3:515], scalar1=cw(3), scalar2=col(C_CB + n),
                                                   op0=ALU.mult, op1=ALU.add), [ut] + PVR, [uc_t])
                for j in range(3):
                    K.op(dve, lambda: V_.scalar_tensor_tensor(out=uc, in0=u[:, j:j + 512], scalar=cw(j), in1=uc,
                                                              op0=ALU.mult, op1=ALU.add), [ut, uc_t] + PVR, [uc_t])

            def s2a(tc):
                L = LS[tc % 2]
                uc, uc_t = L["uc"], L["uc_t"]
                K.mm(ps_t[7], ps[7], [(gm[:, 0, :], uc)], [gmt, uc_t])
                K.mm(ps_t[4], ps[4], [(gm[:, 1, :], uc)], [gmt, uc_t])
                K.op(act, lambda: S_.activation(out=L["thr"], in_=ps[7], func=AF.Tanh, bias=col(C_HBRA + n), scale=0.5),
                     [ps_t[7]] + PVR, [L["thr_t"]])
                K.op(act, lambda: S_.activation(out=L["thi"], in_=ps[4], func=AF.Tanh, bias=col(C_HBRI + n), scale=0.5),
                     [ps_t[4]] + PVR, [L["thi_t"]])
                K.op(act, lambda: S_.activation(out=L["aa"], in_=L["thr"], func=AF.Exp, bias=col(C_HSC1 + n),
                                                scale=col(C_HSC1 + n)), [L["thr_t"]] + PVR, [L["aa_t"]])
                K.op(act, lambda: S_.activation(out=L["tx"], in_=L["thr"], func=AF.Tanh, bias=col(C_NHSC1 + n),
                                                scale=col(C_NHSC1 + n)), [L["thr_t"]] + PVR, [L["tx_t"]])
                K.op(dve, lambda: V_.tensor_tensor(out=L["thr"], in0=L["aa"], in1=L["aa"], op=ALU.mult),
                     [L["aa_t"]], [L["thr_t"]])
                K.op(dve, lambda: V_.scalar_tensor_tensor(out=L["tx"], in0=L["thr"], scalar=1.0, in1=L["tx"], op0=ALU.add,
                                                          op1=ALU.mult), [L["thr_t"], L["tx_t"]], [L["tx_t"]])

            def s2sqrt(tc):
                L = LS[tc % 2]
                K.op(act, lambda: S_.activation(out=L["tx"], in_=L["tx"], func=AF.Sqrt), [L["tx_t"]], [L["tx_t"]])

            def s2b(tc):
                L = LS[tc % 2]
                K.op(dve, lambda: V_.scalar_tensor_tensor(out=L["thi"], in0=L["thi"], scalar=1.0, in1=L["uc"], op0=ALU.add,
                                                          op1=ALU.mult), [L["thi_t"], L["uc_t"]], [L["thi_t"]])
                sc_ = col(C_HPM) if tc < 2 else 0.5
                K.op(dve, lambda: V_.scalar_tensor_tensor(out=L["tx"], in0=L["thi"], scalar=sc_, in1=L["tx"], op0=ALU.mult,
                                                          op1=ALU.mult), [L["thi_t"], L["tx_t"]] + PVR, [L["tx_t"]])
                hcur, hct = hs[tc % 2], hs_t[tc % 2]
                if tc == 0:
                    K.op(dve, lambda: V_.tensor_tensor_scan(out=hcur, data0=L["aa"], data1=L["tx"], initial=0.0,
                                                            op0=ALU.mult, op1=ALU.add), [L["aa_t"], L["tx_t"]], [hct])
                else:
                    hp = hs[(tc - 1) % 2]
                    K.op(dve, lambda: V_.tensor_tensor_scan(out=hcur, data0=L["aa"], data1=L["tx"], initial=hp[:, 511:512],
                                                            op0=ALU.mult, op1=ALU.add),
                         [L["aa_t"], L["tx_t"], hs_t[(tc - 1) % 2]], [hct])

            def s3a(tc):
                k_ = tc % 2
                pb = 6 + k_
                K.mm(ps_t[pb], ps[pb], [(wL[:, kc, 128:256], hsl(kc, tc)) for kc in range(16)], [wt] + hts(tc))
                K.op(act, lambda: S_.activation(out=g1[k_], in_=ps[pb], func=AF.Square), [ps_t[pb]], [g1_t[k_]])
                K.op(dve, lambda: V_.tensor_scalar(out=g1[k_], in0=g1[k_], scalar1=0.044715, scalar2=1.0, op0=ALU.mult,
                                                   op1=ALU.add), [g1_t[k_]], [g1_t[k_]])
                K.op(dve, lambda: V_.tensor_tensor(out=g1[k_], in0=g1[k_], in1=ps[pb], op=ALU.mult),
                     [g1_t[k_], ps_t[pb]], [g1_t[k_]])
                K.op(act, lambda: S_.activation(out=g1[k_], in_=g1[k_], func=AF.Tanh, scale=GC), [g1_t[k_]], [g1_t[k_]])
                K.op(dve, lambda: V_.scalar_tensor_tensor(out=g2[k_], in0=g1[k_], scalar=1.0, in1=ps[pb], op0=ALU.add,
                                                          op1=ALU.mult), [g1_t[k_], ps_t[pb]], [g2_t[k_]])

            def s3b(tc):
                k_ = tc % 2
                K.op(dve, lambda: V_.scalar_tensor_tensor(out=yT[:, n, (tc - 2) * 512:(tc - 1) * 512], in0=g2[k_], scalar=col(C_HGLRU + n),
                                                          in1=hs[k_], op0=ALU.mult, op1=ALU.mult),
                     [g2_t[k_], hs_t[k_]] + PVR, [yT_t[n]])

            seq = [(s1, 0), (s1, 1), (s2a, 0), (s2a, 1), (s2sqrt, 0), (s2sqrt, 1), (s2b, 0), (s2b, 1),
                   (s1, 2), (s1, 3), (s3a, 2), (s2a, 2), (s2a, 3), (s3a, 3), (s2sqrt, 2), (s2sqrt, 3), (s2b, 2), (s3b, 2),
                   (s2b, 3), (s3b, 3)]
            return [(lambda f=f, a=a: f(a)) for f, a in seq]

        n_units = 8 if stop_after != "1b1" else 1
        ldA = {0: ldA0}
        ldL = {0: load_lru(0)}
        for p_ in range(n_units):
            if p_ + 1 < n_units:
                ldA[p_ + 1] = load_attn(p_ + 1)
                ldL[p_ + 1] = load_lru(p_ + 1)
            stages = (fc_stages() if p_ == 0 else []) + lru_unit(p_, ldL.pop(p_))
            attention_unit(p_, ldA.pop(p_), extra=stages)
        pending = []
        def tok_sumsq(srcT, src_t, rg0, s_idx):
            for c in range(8):
                for tc in range(2):
                    si = nxt("hnbuf", 2)
                    sqb, sqb_t = HN["sq"][si], HN["sq_t"][si]
                    K.op(act, lambda: S_.activation(out=sqb, in_=srcT[:, c, tc * 512:(tc + 1) * 512], func=AF.Square,
                                                    scale=col(rg0 + c)), [src_t[c]] + PVR, [sqb_t])
                    K.mm_multi(ps_t[5], [(ps[5][:, (s_idx * 8 + tc * 4 + b4) * 8:(s_idx * 8 + tc * 4 + b4 + 1) * 8],
                                          [(sqb[:, b4 * 128:(b4 + 1) * 128], onesb[:, 0:8])]) for b4 in range(4)],
                               [sqb_t, cst_t], start=(c == 0 and tc == 0), stop=(c == 7), first_only=True)

        if n_units == 8:
            tok_sumsq(oT, oT_t, C_RGFOX, 0)
        for st_ in pending:
            st_()
        if n_units == 8:
            tok_sumsq(yT, yT_t, C_RGLRU, 1)
        dump("oT", oT, [128, 8, OWN], BF16, oT_t)
        dump("yT", yT, [128, 8, OWN], BF16, yT_t)

        def finish():
            deps = [(t.dsem, t.dcnt) for t in K.dma_tiles if t.dcnt > 0]
            deps += [(e.sem, e.cnt) for e in K.engs if e.cnt > 0]
            K._wait(sp, deps)

        if stop_after in ("1a", "1b", "1b1"):
            finish()
            return nc, dbg_out

        RH.reset(0)
        xtok = RH.alloc([128, 8, D], F32); xtok_t = [Tile(f"xtok{i}") for i in range(8)]

        def merge_r(dst, sv):
            k_ = id(sv[0])
            if k_ not in dst.r or dst.r[k_][1] < sv[1]:
                dst.r[k_] = sv

        for t_ in xtok_t:
            for ht in hT_t:
                for sv in ht.r.values():
                    merge_r(t_, sv)
                if ht.w is not None:
                    merge_r(t_, ht.w)
        wo_bufs = [(wbufA[0][:, 0:4096], wbufA_t[0]), (wbufL[0], wbufL_t[0]), (wbufA[1][:, 0:4096], wbufA_t[1]),
                   (wbufL[1], wbufL_t[1])]
        WOc = []
        for c8 in range(8):
            buf, bt = wo_bufs[c8 % 4]
            WOc.append((buf.rearrange("p (a b) -> p a b", a=16), bt))

        def load_wo(c8):
            K.dma_in(pool, WOc[c8][1], WOc[c8][0], w_out_v[:, :, c8 * 256:(c8 + 1) * 256])

        for c8 in range(4):
            load_wo(c8)
        for tb in range(8):
            K.dma_in(sp, xtok_t[tb], xtok[:, tb, :], xin[(8 + tb) * 128:(9 + tb) * 128, :])
        rs16, rs16_t = lneg.rearrange("p a b -> p (a b)"), lneg_t
        K.op(act, lambda: S_.activation(out=rs16, in_=ps[5][:, 0:128], func=AF.Sqrt, scale=1.0 / 1024, bias=col(C_EPS)),
             [ps_t[5]] + PVR, [rs16_t])
        K.op(dve, lambda: V_.reciprocal(out=rs16, in_=rs16), [rs16_t], [rs16_t])
        dump("rs16", rs16, [128, 128], F32, [rs16_t])

        for c8 in range(8):
            Wc, Wc_t = WOc[c8]
            for tb in range(8):
                pb = nxt("pacc4", 4)
                K.mm_multi(ps_t[pb], [(ps[pb][:, 0:256], [(oT[:, kc, tb * 128:(tb + 1) * 128], Wc[:, kc, :]) for kc in range(8)]),
                                      (ps[pb][:, 256:512], [(yT[:, kc, tb * 128:(tb + 1) * 128], Wc[:, 8 + kc, :]) for kc in range(8)])],
                           [Wc_t] + oT_t + yT_t)
                xs = xtok[:, tb, c8 * 256:(c8 + 1) * 256]
                K.op(dve, lambda: V_.scalar_tensor_tensor(out=xs, in0=ps[pb][:, 0:256], scalar=rs16[:, tb * 8:tb * 8 + 1], in1=xs,
                                                          op0=ALU.mult, op1=ALU.add), [ps_t[pb], xtok_t[tb], rs16_t], [xtok_t[tb]])
                K.op(dve, lambda: V_.scalar_tensor_tensor(out=xs, in0=ps[pb][:, 256:512], scalar=rs16[:, (8 + tb) * 8:(8 + tb) * 8 + 1], in1=xs,
                                                          op0=ALU.mult, op1=ALU.add), [ps_t[pb], xtok_t[tb], rs16_t], [xtok_t[tb]])
            if c8 + 4 < 8:
                load_wo(c8 + 4)
        dump("x1", xtok, [128, 8, D], F32, xtok_t)
        if stop_after == "1c":
            finish()
            return nc, dbg_out

        K.barrier()
        RX.reset(0)
        hT2 = RX.alloc([128, 16, OWN], BF16); hT2_t = [Tile(f"hT2_{i}") for i in range(8)]
        Wcq = RX.alloc([128, 16, 512], BF16); Wcq_t = Tile("Wcq")
        gbc = RX.alloc([128, D], F32); gbc_t = Tile("gbc2")
        hn = [RX.alloc([128, D], BF16) for _ in range(2)]; hn_t = [Tile(f"hn2_{i}") for i in range(2)]
        RS.reset(0)
        mnT = RS.alloc([128, 16, 256], BF16); mnT_t = [Tile(f"mnT{i}") for i in range(2)]
        Wck = RS.alloc([128, 16, 512], BF16); Wck_t = Tile("Wck")
        Wcv = RS.alloc([128, 16, 512], BF16); Wcv_t = Tile("Wcv")
        ckT = RS.alloc([128, 4, 256], BF16); ckT_t = Tile("ckT")
        cv = RS.alloc([128, 2, 512], BF16); cv_t = Tile("cv")
        cq_raw = RS.alloc([128, D], F32)
        ox_raw = RS.alloc([128, D], F32)
        cqT = cq_raw.bitcast(BF16).rearrange("p (a b) -> p a b", a=4); cqT_t = [Tile(f"cqT{i}") for i in range(4)]
        oxT = ox_raw.bitcast(BF16).rearrange("p (a b) -> p a b", a=4); oxT_t = [Tile(f"oxT{i}") for i in range(4)]
        PT = [RS.alloc([128, 512], BF16) for _ in range(3)]; PT_t = [Tile(f"PT2_{i}") for i in range(3)]
        HN["sq"] = [RS.alloc([128, 512], BF16) for _ in range(2)]; HN["sq_t"] = [Tile(f"sq2_{i}") for i in range(2)]
        HN["sr"] = [RS.alloc([128, 512], F32) for _ in range(2)]; HN["sr_t"] = [Tile(f"sr2_{i}") for i in range(2)]
        stat = [RS.alloc([128, 8], F32) for _ in range(2)]; stat_t = [Tile(f"stat2_{i}") for i in range(2)]

        K.dma_in(pool, Wck_t, Wck, w_ckv_v[:, :, 0:512])
        K.dma_in(pool, Wcv_t, Wcv, w_ckv_v[:, :, 512:1024])
        K.dma_in(pool, Wcq_t, Wcq, w_cq_v[:, :, :])
        gbc2 = gbc
        K.dma_in(sp, gbc_t, gbc2, grow[1:2, :].partition_broadcast(128))
        memst_t = [Tile("memst0"), Tile("gmem_bc")]
        K.dma_in(sp, memst_t[1], ox_raw, grow[2:3, :].partition_broadcast(128))
        K.dma_in(sp, memst_t[0], cq_raw, memb[0:128, :])
        def nr2(tb):
            i = tb % 2
            norm_rows(xtok_t[tb], xtok[:, tb, :], gbc_t, gbc2, hn[i], stat[i], stat_t[i], hn[i], hn_t[i])

        nr2(0)
        for tb in range(8):
            if tb + 1 < 8:
                nr2(tb + 1)
            transpose_rows(hn_t[tb % 2], hn[tb % 2], hT2, hT2_t[tb], tb * 128)

        for mb in range(2):
            if mb == 1:
                K.dma_in(sp, memst_t[0], cq_raw, memb[128:256, :])
            norm_rows(memst_t[0], cq_raw, memst_t[1], ox_raw, hn[mb], stat[mb], stat_t[mb], hn[mb], hn_t[mb])
            transpose_rows(hn_t[mb], hn[mb], mnT, mnT_t[mb], mb * 128)
        rsck = RS.alloc([128, 64], F32); rsck_t = Tile("rsck")
        for h in range(4):
            pb = nxt("pacc", 2)
            K.mm(ps_t[pb], ps[pb][:, 0:256], [(Wck[:, kc, h * 128:(h + 1) * 128], mnT[:, kc, :]) for kc in range(16)],
                 [Wck_t] + mnT_t)
            k_ = nxt("hnbuf", 2)
            sq, sq_t = HN["sq"][k_], HN["sq_t"][k_]
            K.op(act, lambda: S_.activation(out=sq[:, 0:256], in_=ps[pb][:, 0:256], func=AF.Square), [ps_t[pb]], [sq_t])
            K.mm_multi(ps_t[5], [(ps[5][:, (h * 2 + mb) * 8:(h * 2 + mb + 1) * 8],
                                  [(sq[:, mb * 128:(mb + 1) * 128], onesb[:, 0:8])]) for mb in range(2)], [sq_t, cst_t])
            K.op(act, lambda: S_.activation(out=ckT[:, h, :], in_=ps[pb][:, 0:256], func=AF.Copy, scale=col(C_GCK)),
                 [ps_t[pb]] + PVR, [ckT_t])
        K.op(act, lambda: S_.activation(out=rsck, in_=ps[5][:, 0:64], func=AF.Sqrt, scale=1.0 / 128, bias=col(C_EPS)),
             [ps_t[5]] + PVR, [rsck_t])
        K.op(dve, lambda: V_.reciprocal(out=rsck, in_=rsck), [rsck_t], [rsck_t])
        for mb in range(2):
            pb = nxt("pacc", 2)
            K.mm(ps_t[pb], ps[pb], [(mnT[:, kc, mb * 128:(mb + 1) * 128], Wcv[:, kc, :]) for kc in range(16)],
                 [Wcv_t, mnT_t[mb]])
            K.op(act, lambda: S_.copy(out=cv[:, mb, :], in_=ps[pb]), [ps_t[pb]], [cv_t])
        for t_ in cqT_t + oxT_t:
            for mt in memst_t:
                t_.r.update(mt.r)
                if mt.w is not None:
                    t_.r[id(mt.w[0])] = mt.w
        def hsl2(kc, tc):
            return hT2[:, kc, tc * 512:(tc + 1) * 512]

        def hts2(tc):
            return hT2_t[4 * tc:4 * tc + 4]

        cq_items = [(h, tc) for h in range(4) for tc in range(2)]
        cq_pb = {}

        def cq_proj(i_):
            h, tc = cq_items[i_]
            pb = nxt("pacc4", 4)
            cq_pb[i_] = pb
            K.mm(ps_t[pb], ps[pb], [(Wcq[:, kc, h * 128:(h + 1) * 128], hsl2(kc, tc)) for kc in range(16)],
                 [Wcq_t] + hts2(tc))

        cq_proj(0)
        cq_proj(1)
        for i_, (h, tc) in enumerate(cq_items):
            if i_ + 2 < len(cq_items):
                cq_proj(i_ + 2)
            headnorm(cq_pb[i_], 512, cqT_t[h], cqT[:, h, tc * 512:(tc + 1) * 512], C_GCQS, 1.0 / 128)
        def xs_mm(h, tc, mb, it):
            sb_ = 2 + (it % 2)
            K.mm(ps_t[sb_], ps[sb_], [(ckT[:, h, mb * 128:(mb + 1) * 128], cqT[:, h, tc * 512:(tc + 1) * 512])],
                 [ckT_t, cqT_t[h]])

        steps = [(h, tc, mb) for h in range(4) for tc in range(2) for mb in range(2)]
        xs_mm(*steps[0], 0)
        for it, (h, tc, mb) in enumerate(steps):
            sb_ = 2 + (it % 2)
            bo, br = (4, 5) if (it // 2) % 2 == 0 else (6, 7)
            if it + 1 < len(steps):
                xs_mm(*steps[it + 1], it + 1)
            pi_ = nxt("PT", 3)
            K.op(act, lambda: S_.activation(out=PT[pi_], in_=ps[sb_], func=AF.Exp,
                                            scale=rsck[:, (h * 2 + mb) * 8:(h * 2 + mb) * 8 + 1]),
                 [ps_t[sb_], rsck_t], [PT_t[pi_]])
            K.mm(ps_t[bo], ps[bo], [(cv[:, mb, h * 128:(h + 1) * 128], PT[pi_])], [cv_t, PT_t[pi_]],
                 start=(mb == 0), stop=(mb == 1))
            K.mm(ps_t[br], ps[br], [(onesb, PT[pi_])], [PT_t[pi_], cst_t], start=(mb == 0), stop=(mb == 1))
            if mb == 1:
                k_ = nxt("hnbuf", 2)
                rinv, rinv_t = HN["sr"][k_], HN["sr_t"][k_]
                K.op(dve, lambda: V_.reciprocal(out=rinv, in_=ps[br]), [ps_t[br]], [rinv_t])
                K.op(dve, lambda: V_.tensor_tensor(out=oxT[:, h, tc * 512:(tc + 1) * 512], in0=ps[bo], in1=rinv, op=ALU.mult),
                     [ps_t[bo], rinv_t], [oxT_t[h]])
        Wco = Wck.rearrange("p a b -> p (a b)")[:, 0:4 * D].rearrange("p (a b) -> p a b", a=4)
        K.dma_in(pool, Wck_t, Wco, w_co_v[:, :, :])
        for fc in range(4):
            for tb in range(8):
                pb = nxt("pacc", 2)
                K.mm(ps_t[pb], ps[pb], [(oxT[:, h, tb * 128:(tb + 1) * 128], Wco[:, h, fc * 512:(fc + 1) * 512])
                                        for h in range(4)], [Wck_t] + oxT_t)
                xs = xtok[:, tb, fc * 512:(fc + 1) * 512]
                K.op(dve, lambda: V_.tensor_tensor(out=xs, in0=ps[pb], in1=xs, op=ALU.add),
                     [ps_t[pb], xtok_t[tb]], [xtok_t[tb]])
        dump("x2", xtok, [128, 8, D], F32, xtok_t)
        if stop_after == "2":
            finish()
            return nc, dbg_out

        K.barrier()
        RX.reset(8192)
        actq = RX.alloc([128, 11, OWN], BF16); actq_t = [Tile(f"actq{i}") for i in range(11)]
        RS.reset(0)
        stat = [RS.alloc([128, 8], F32) for _ in range(2)]; stat_t = [Tile(f"stat3_{i}") for i in range(2)]
        Wgu = [RS.alloc([128, 16, 256], BF16) for _ in range(4)]; Wgu_t = [Tile(f"Wgu{i}") for i in range(4)]
        Wd = [RS.alloc([128, 11, 256], BF16) for _ in range(3)]; Wd_t = [Tile(f"Wd{i}") for i in range(3)]
        sg = [RS.alloc([128, 512], F32) for _ in range(2)]; sg_t = [Tile(f"sg{i}") for i in range(2)]
        gbc = RS.alloc([128, D], F32); gbc_t = Tile("gbc3")
        hn = [RS.alloc([128, D], BF16) for _ in range(2)]; hn_t = [Tile(f"hn3_{i}") for i in range(2)]
        hT2_t = [Tile(f"hT3_{i}") for i in range(8)]
        K.dma_in(sp, gbc_t, gbc, grow[3:4, :].partition_broadcast(128))
        hn4 = hn + [Wd[0].rearrange("p a b -> p (a b)")[:, 0:D], Wd[1].rearrange("p a b -> p (a b)")[:, 0:D]]
        hn4_t = hn_t + [Wd_t[0], Wd_t[1]]
        stat4 = [stat[0][:, 0:4], stat[0][:, 4:8], stat[1][:, 0:4], stat[1][:, 4:8]]
        stat4_t = [Tile(f"stat43_{i}") for i in range(4)]
        def nr3(tb):
            i = tb % 4
            norm_rows(xtok_t[tb], xtok[:, tb, :], gbc_t, gbc, hn4[i], stat4[i], stat4_t[i], hn4[i], hn4_t[i])

        for tb in range(2):
            nr3(tb)
        for tb in range(8):
            if tb + 2 < 8:
                nr3(tb + 2)
            transpose_rows(hn4_t[tb % 4], hn4[tb % 4], hT2, hT2_t[tb], tb * 128)
        for q in range(4):
            hc0 = q * 11
            hl = 0
            for gsz in (2, 2, 2, 2, 2, 1):
                gi_, ui_ = nxt("Wgu", 4), nxt("Wgu", 4)
                c0 = (hc0 + hl) * 128
                K.dma_in(pool, Wgu_t[gi_], Wgu[gi_][:, :, 0:gsz * 128], w_gu_v[:, :, c0:c0 + gsz * 128])
                K.dma_in(pool, Wgu_t[ui_], Wgu[ui_][:, :, 0:gsz * 128], w_gu_v[:, :, FFN + c0:FFN + c0 + gsz * 128])
                for g_ in range(gsz):
                    for tc in range(2):
                        pg, pu = tc, 2 + tc
                        K.mm(ps_t[pg], ps[pg], [(Wgu[gi_][:, kc, g_ * 128:(g_ + 1) * 128], hsl2(kc, tc)) for kc in range(16)],
                             [Wgu_t[gi_]] + hts2(tc))
                        K.mm(ps_t[pu], ps[pu], [(Wgu[ui_][:, kc, g_ * 128:(g_ + 1) * 128], hsl2(kc, tc)) for kc in range(16)],
                             [Wgu_t[ui_]] + hts2(tc))
                        si = nxt("sg", 2)
                        K.op(act, lambda: S_.activation(out=sg[si], in_=ps[pg], func=AF.Silu), [ps_t[pg]], [sg_t[si]])
                        K.op(dve, lambda: V_.tensor_tensor(out=actq[:, hl, tc * 512:(tc + 1) * 512], in0=sg[si], in1=ps[pu],
                                                           op=ALU.mult), [sg_t[si], ps_t[pu]], [actq_t[hl]])
                    hl += 1
            for f8 in range(8):
                di = nxt("Wd", 3)
                K.dma_in(pool, Wd_t[di], Wd[di], w_down_v[:, hc0:hc0 + 11, f8 * 256:(f8 + 1) * 256])
                for tb in range(8):
                    pd = 4 + nxt("pd", 4)
                    K.mm(ps_t[pd], ps[pd][:, 0:256],
                         [(actq[:, hl_, tb * 128:(tb + 1) * 128], Wd[di][:, hl_, :]) for hl_ in range(11)],
                         [Wd_t[di]] + actq_t)
                    xs = xtok[:, tb, f8 * 256:(f8 + 1) * 256]
                    K.op(dve, lambda: V_.tensor_tensor(out=xs, in0=ps[pd][:, 0:256], in1=xs, op=ALU.add),
                         [ps_t[pd], xtok_t[tb]], [xtok_t[tb]])
                if q == 3:
                    K._wait(sp, K._deps(sp, xtok_t, []))
                    st_t = Tile(f"ost{f8}")
                    K.dma_out(sp, st_t, out_v[:, :, f8 * 256:(f8 + 1) * 256], xtok[:, :, f8 * 256:(f8 + 1) * 256])
        finish()
    return nc, dbg_out


_NC_CACHE = {}


def _get_nc():
    return build()[0]


def make_in_maps(inp):
    f = lambda a: np.ascontiguousarray(np.asarray(a, dtype=np.float32))
    x, mem = f(inp["x"]), f(inp["mem"])
    pv = np.zeros((128, NV), np.float32)
    pv[:, C_GQ] = f(inp["g_q"])[0]
    pv[:, C_GK] = f(inp["g_k"])[0]
    pv[:, C_GCQ] = f(inp["g_cq"])[0]
    pv[:, C_GCK] = f(inp["g_ck"])[0]
    cwt = f(inp["conv_w"])[0]
    for j in range(4):
        pv[:, C_CW + j * 8:C_CW + j * 8 + 8] = cwt[j].reshape(8, 128).T
    for c0, key in ((C_CB, "conv_b"), (C_BRA, "b_ra"), (C_BRI, "b_ri"), (C_LAM, "lam"), (C_GFOX, "g_fox_out"),
                    (C_GLRU, "g_lru_out")):
        pv[:, c0:c0 + 8] = f(inp[key])[0].reshape(8, 128).T
    grow = np.stack([f(inp["g_mix"])[0], f(inp["g_xattn"])[0], f(inp["g_mem"])[0], f(inp["g_ffn"])[0]], 0)
    bfr = np.tile(f(inp["b_f"])[0], 16)[None, :]
    shared = {
        "grow": np.ascontiguousarray(grow), "bfr": np.ascontiguousarray(bfr),
        "w_in": f(inp["w_in"])[0], "w_ra": f(inp["w_ra"])[0], "w_ri": f(inp["w_ri"])[0], "w_out": f(inp["w_out"])[0],
        "w_cq": f(inp["w_cq"])[0], "w_ckv": f(inp["w_ckv"])[0], "w_co": f(inp["w_co"])[0],
        "w_gu": f(inp["w_gate_up"])[0], "w_down": f(inp["w_down"])[0],
    }
    maps = []
    for c in range(8):
        b, j = c // 2, c % 2
        if j == 1:
            xin = x[b]
        else:
            xin = np.zeros((NTOK, D), np.float32)
            xin[OWN:] = x[b, :OWN]
        pvc = pv.copy()
        pvc[:, C_PM] = float(j)
        m = dict(shared)
        m.update({"xin": np.ascontiguousarray(xin), "memb": np.ascontiguousarray(mem[b]), "pvec": pvc})
        maps.append(m)
    return maps


def kernel(**inputs):
    nc = _get_nc()
    maps = make_in_maps(inputs)
    res = run_bass_kernel_spmd(nc, maps, core_ids=list(range(8)))
    out = np.empty((4, 2048, D), np.float32)
    for c in range(8):
        b, j = c // 2, c % 2
        out[b, j * OWN:(j + 1) * OWN] = res.results[c]["out"]
    return out
```
